# Optimizing a Trainium2 kernel written in Bass

```python
import jax, jax.numpy as jnp
from jax import lax
import numpy as np

D_MODEL = 1024
BATCH = 2
SEQ = 8192
DEPTH = 1

EPS = 1e-6
D_FF = 2816
N_MOD = 9
GLA_HEADS = 4
GLA_DK = D_MODEL // 2
GLA_DV = D_MODEL
GLA_HK = GLA_DK // GLA_HEADS
GLA_HV = GLA_DV // GLA_HEADS
GLA_RANK = 16
GLA_TAU = 16.0
GLA_CHUNK = 64
FOX_HEADS = 16
FOX_HD = D_MODEL // FOX_HEADS
FOX_W = FOX_HEADS * FOX_HD
Q_BLOCK = 128
SPLITS = (GLA_DK, GLA_DK, GLA_DV, GLA_RANK, GLA_DV, FOX_W, FOX_W, FOX_W, FOX_HEADS, D_MODEL, D_MODEL)
IN_COLS = 2 * GLA_DK + 2 * GLA_DV + GLA_RANK + 3 * FOX_W + FOX_HEADS + 2 * D_MODEL

kernel_name = "hybrid_gla_fox_macaron_adaln"


def rmsnorm(x, g):
    xf = x.astype(jnp.float32)
    y = xf * lax.rsqrt(jnp.mean(xf * xf, axis=-1, keepdims=True) + EPS)
    return (y * g.astype(jnp.float32)).astype(x.dtype)


def swiglu(h, w_gu, w_dn):
    g, u = jnp.split(h @ w_gu, 2, axis=-1)
    return (jax.nn.silu(g) * u) @ w_dn


def gla_chunked(q, k, v, log_a):
    B, H, T, dk = q.shape
    dv = v.shape[-1]
    C = GLA_CHUNK
    N = T // C

    def to_chunks(a):
        return a.reshape(B, H, N, C, a.shape[-1]).transpose(2, 0, 1, 3, 4)

    qc, kc, vc, gc = to_chunks(q), to_chunks(k), to_chunks(v), to_chunks(log_a)
    causal = jnp.tril(jnp.ones((C, C), dtype=bool))[:, :, None]

    def step(S, inp):
        qi, ki, vi, gi = inp
        b = jnp.cumsum(gi, axis=2)
        o_inter = jnp.einsum('bhtd,bhde->bhte', qi * jnp.exp(b), S)
        diff = b[:, :, :, None, :] - b[:, :, None, :, :]
        decay = jnp.exp(jnp.where(causal, diff, -jnp.inf))
        A = jnp.einsum('bhtd,bhsd,bhtsd->bhts', qi, ki, decay)
        o = o_inter + jnp.einsum('bhts,bhse->bhte', A, vi)
        b_last = b[:, :, -1:, :]
        S_new = jnp.exp(b_last[:, :, 0, :, None]) * S + jnp.einsum(
            'bhsd,bhse->bhde', ki * jnp.exp(b_last - b), vi)
        return S_new, o

    S0 = jnp.zeros((B, H, dk, dv), jnp.float32)
    _, o = lax.scan(step, S0, (qc, kc, vc, gc))
    return o.transpose(1, 2, 0, 3, 4).reshape(B, H, T, dv)


def fox_attention(q, k, v, logf):
    B, H, T, hd = q.shape
    F = jnp.cumsum(logf, axis=-1)
    nb = T // Q_BLOCK
    qb = q.reshape(B, H, nb, Q_BLOCK, hd).transpose(2, 0, 1, 3, 4)
    Fb = F.reshape(B, H, nb, Q_BLOCK).transpose(2, 0, 1, 3)
    kpos = jnp.arange(T)
    scale = hd ** -0.5

    def block(args):
        qi, Fi, i = args
        s = jnp.einsum('bhqd,bhkd->bhqk', qi, k).astype(jnp.float32) * scale
        s = s + Fi[..., None] - F[:, :, None, :]
        qpos = i * Q_BLOCK + jnp.arange(Q_BLOCK)
        s = jnp.where(kpos[None, :] <= qpos[:, None], s, -jnp.inf)
        p = jax.nn.softmax(s, axis=-1)
        return jnp.einsum('bhqk,bhkd->bhqd', p.astype(v.dtype), v)

    o = lax.map(block, (qb, Fb, jnp.arange(nb)))
    return o.transpose(1, 2, 0, 3, 4).reshape(B, H, T, hd)


def token_mixer(h, w_in, w_a2, b_a, b_f, g_gla, w_pa, w_pb, w_out):
    B, T, _ = h.shape
    f32 = jnp.float32
    z = h @ w_in
    cuts = np.cumsum(SPLITS)[:-1].tolist()
    q_a, k_a, v_a, a_low, r_a, q_b, k_b, v_b, f_b, gate_a, gate_b = jnp.split(z, cuts, axis=-1)

    def heads(t, n):
        return t.reshape(B, T, n, -1).transpose(0, 2, 1, 3)

    log_a = jax.nn.log_sigmoid((a_low @ w_a2 + b_a).astype(f32)) / GLA_TAU
    o_a = gla_chunked(heads(q_a.astype(f32) * (GLA_HK ** -0.5), GLA_HEADS),
                      heads(k_a.astype(f32), GLA_HEADS),
                      heads(v_a.astype(f32), GLA_HEADS),
                      heads(log_a, GLA_HEADS))
    o_a = rmsnorm(o_a.transpose(0, 2, 1, 3), g_gla.reshape(GLA_HEADS, GLA_HV))
    o_a = o_a.astype(h.dtype).reshape(B, T, GLA_DV) * jax.nn.silu(r_a)

    logf = jax.nn.log_sigmoid((f_b + b_f).astype(f32)).transpose(0, 2, 1)
    o_b = fox_attention(heads(q_b, FOX_HEADS), heads(k_b, FOX_HEADS), heads(v_b, FOX_HEADS), logf)
    o_b = o_b.transpose(0, 2, 1, 3).reshape(B, T, FOX_W)

    merged = jax.nn.sigmoid(gate_a) * (o_a @ w_pa) + jax.nn.sigmoid(gate_b) * (o_b @ w_pb)
    return merged @ w_out


def setup_inputs(seed: int = 0) -> dict:
    key = jax.random.key(seed)
    ks = jax.random.split(key, 20)
    L, D = DEPTH, D_MODEL
    nrm = jax.random.normal

    def w(k, shape, fan_in):
        return nrm(k, shape, jnp.float32) * fan_in ** -0.5

    return {
        "x": nrm(ks[0], (BATCH, SEQ, D), jnp.float32),
        "c": nrm(ks[1], (BATCH, D), jnp.float32),
        "w_ada": w(ks[2], (L, D, N_MOD * D), D) * 0.5,
        "b_ada": 0.02 * nrm(ks[3], (L, N_MOD * D), jnp.float32),
        "g_pre": 1.0 + 0.05 * nrm(ks[4], (L, 3, D), jnp.float32),
        "g_post": 1.0 + 0.05 * nrm(ks[5], (L, 3, D), jnp.float32),
        "w_gu1": w(ks[6], (L, D, 2 * D_FF), D),
        "w_dn1": w(ks[7], (L, D_FF, D), D_FF),
        "w_gu2": w(ks[8], (L, D, 2 * D_FF), D),
        "w_dn2": w(ks[9], (L, D_FF, D), D_FF),
        "w_in": w(ks[10], (L, D, IN_COLS), D),
        "w_a2": w(ks[11], (L, GLA_RANK, GLA_DK), GLA_RANK),
        "b_a": 0.1 * nrm(ks[12], (L, GLA_DK), jnp.float32),
        "b_f": 1.0 + 0.1 * nrm(ks[13], (L, FOX_HEADS), jnp.float32),
        "g_gla": 1.0 + 0.05 * nrm(ks[14], (L, GLA_DV), jnp.float32),
        "w_pa": w(ks[15], (L, GLA_DV, D), GLA_DV),
        "w_pb": w(ks[16], (L, FOX_W, D), FOX_W),
        "w_out": w(ks[17], (L, D, D), D),
    }


def reference(x, c, w_ada, b_ada, g_pre, g_post, w_gu1, w_dn1, w_gu2, w_dn2,
              w_in, w_a2, b_a, b_f, g_gla, w_pa, w_pb, w_out):
    B = x.shape[0]
    for l in range(DEPTH):
        mods = (jax.nn.silu(c) @ w_ada[l] + b_ada[l]).reshape(B, N_MOD, 1, D_MODEL)
        sh1, sc1, gt1, sh2, sc2, gt2, sh3, sc3, gt3 = [mods[:, i] for i in range(N_MOD)]

        h = rmsnorm(x, g_pre[l, 0]) * (1 + sc1) + sh1
        x = x + 0.5 * gt1 * rmsnorm(swiglu(h, w_gu1[l], w_dn1[l]), g_post[l, 0])

        h = rmsnorm(x, g_pre[l, 1]) * (1 + sc2) + sh2
        y = token_mixer(h, w_in[l], w_a2[l], b_a[l], b_f[l], g_gla[l], w_pa[l], w_pb[l], w_out[l])
        x = x + gt2 * rmsnorm(y, g_post[l, 1])

        h = rmsnorm(x, g_pre[l, 2]) * (1 + sc3) + sh3
        x = x + 0.5 * gt3 * rmsnorm(swiglu(h, w_gu2[l], w_dn2[l]), g_post[l, 2])
    return x
```

```python
import numpy as np
import concourse.bass as bass
import concourse.mybir as mybir
from concourse.bass_utils import run_bass_kernel_spmd

F32 = mybir.dt.float32
BF16 = mybir.dt.bfloat16
AF = mybir.ActivationFunctionType
ALU = mybir.AluOpType

D = 1024
KC = 8
NTOK = 2048
NT = 16
DFF = 2816
HC = 22
EPS = 1e-6
NCORES = 8
GRP = 4
W_IN_COLS = 8224
O_QA, O_KA, O_VA, O_AL, O_RA, O_QB, O_KB, O_VB, O_FB, O_GA, O_GB = (
    0, 512, 1024, 2048, 2064, 3088, 4112, 5136, 6160, 6176, 7200)
KILL = -30000.0


class Sched:
    ENGS = ("pe", "act", "dve", "pool", "sp")

    def __init__(self, nc):
        self.nc = nc
        self.ops = []
        self.last_writer = {}
        self.readers = {}
        self.eng_ops = {e: [] for e in self.ENGS}
        self.dma_groups = {}
        self.barriers = []
        self.bar_skip = []

    muted = False
    attach_waits = True

    def op(self, eng, fn, r=(), w=(), dma=None, inc=16):
        if self.muted:
            return None
        idx = len(self.ops)
        deps = set()
        for k in r:
            lw = self.last_writer.get(k)
            if lw is not None:
                deps.add(lw)
        for k in w:
            lw = self.last_writer.get(k)
            if lw is not None:
                deps.add(lw)
            for ridx in self.readers.get(k, {}).values():
                deps.add(ridx)
        if dma is not None:
            g = self.dma_groups.setdefault(dma, [])
            if g:
                deps.add(g[-1])
            g.append(idx)
        for k in w:
            self.last_writer[k] = idx
            self.readers[k] = {}
        rk = ("dma", idx) if dma is not None else eng
        for k in r:
            self.readers.setdefault(k, {})[rk] = idx
        deps.discard(idx)
        o = dict(eng=eng, fn=fn, deps=deps, dma=dma, inc=inc, signal=False,
                 pos=len(self.eng_ops[eng]), bar=len(self.barriers))
        self.ops.append(o)
        self.eng_ops[eng].append(idx)
        return idx

    def barrier(self, skip_prefix=None):
        self.barriers.append(len(self.ops))
        self.bar_skip.append(skip_prefix)

    def _needs_sync(self, o, d):
        if d["dma"] is not None:
            return True
        if d["eng"] != o["eng"]:
            return True
        if o["eng"] == "pe":
            return False
        if o["dma"] is not None:
            return True
        return (o["pos"] - d["pos"]) <= 3

    def emit(self):
        nc = self.nc
        ops = self.ops
        bar_last = []
        for bpos in self.barriers:
            last = {}
            for e in self.ENGS:
                cand = [i for i in self.eng_ops[e] if i < bpos and ops[i]["dma"] is None]
                if cand:
                    last[e] = cand[-1]
            dl = {}
            skip = self.bar_skip[len(bar_last)]
            for gname, lst in self.dma_groups.items():
                if skip is not None and gname.startswith(skip):
                    continue
                cand = [i for i in lst if i < bpos]
                if cand:
                    dl[gname] = cand[-1]
            bar_last.append((last, dl))
        for o in ops:
            for d in o["deps"]:
                if ops[d]["dma"] is None and self._needs_sync(o, ops[d]):
                    ops[d]["signal"] = True
        for last, _ in bar_last:
            for e, i in last.items():
                ops[i]["signal"] = True
        cnt = {e: 0 for e in self.ENGS}
        for o in ops:
            if o["dma"] is None and o["signal"]:
                cnt[o["eng"]] += 1
                o["val"] = cnt[o["eng"]]
        for gname, lst in self.dma_groups.items():
            v = 0
            for i in lst:
                v += ops[i]["inc"]
                ops[i]["val"] = v
        import contextlib
        stack = contextlib.ExitStack()
        sems = {}
        for e in self.ENGS:
            sems[e] = stack.enter_context(nc.semaphore("s_" + e))
        gsem = {}
        for gi, gname in enumerate(self.dma_groups):
            gsem[gname] = stack.enter_context(nc.semaphore("g%d" % gi))
        engobj = {"pe": nc.tensor, "act": nc.scalar, "dve": nc.vector,
                  "pool": nc.gpsimd, "sp": nc.sync}
        final_waits = []
        for gname, lst in self.dma_groups.items():
            final_waits.append((gsem[gname], ops[lst[-1]]["val"]))

        def run_engine(ename, eng):
            waited = {}

            def wait(sem, val, key):
                if waited.get(key, 0) >= val:
                    return
                waited[key] = val
                eng.wait_ge(sem, val)

            nbar = 0
            for idx in self.eng_ops[ename]:
                o = ops[idx]
                while nbar < o["bar"]:
                    last, dl = bar_last[nbar]
                    for e, i in last.items():
                        if e != ename or ename != "pe":
                            wait(sems[e], ops[i]["val"], e)
                    for gname, i in dl.items():
                        wait(gsem[gname], ops[i]["val"], "g" + gname)
                    nbar += 1
                need = {}
                for d in sorted(o["deps"]):
                    dd = ops[d]
                    if not self._needs_sync(o, dd):
                        continue
                    if dd["dma"] is not None:
                        key, sem = "g" + dd["dma"], gsem[dd["dma"]]
                    else:
                        key, sem = dd["eng"], sems[dd["eng"]]
                    if waited.get(key, 0) >= dd["val"]:
                        continue
                    if key not in need or need[key][1] < dd["val"]:
                        need[key] = (sem, dd["val"])
                attach = None
                if self.attach_waits and need and o["dma"] is None:
                    k_last = sorted(need)[-1]
                    attach = need.pop(k_last)
                    waited[k_last] = attach[1]
                for key in sorted(need):
                    wait(need[key][0], need[key][1], key)
                ins = o["fn"](eng)
                if attach is not None:
                    ins._wait_ge(attach[0], attach[1])
                if o["dma"] is not None:
                    ins.then_inc(gsem[o["dma"]], o["inc"])
                elif o["signal"]:
                    ins.then_inc(sems[ename], 1)
            if ename == "sp":
                for sem, val in final_waits:
                    eng.wait_ge(sem, val)

        with stack:
            with nc.Block() as block:
                @block.tensor
                def _(e):
                    run_engine("pe", e)

                @block.scalar
                def _(e):
                    run_engine("act", e)

                @block.vector
                def _(e):
                    run_engine("dve", e)

                @block.gpsimd
                def _(e):
                    run_engine("pool", e)

                @block.sync
                def _(e):
                    run_engine("sp", e)


class Builder:
    def __init__(self, debug=(), stop_after=None, skip=()):
        self.skip = set(skip)
        self.cut = 0
        self.v = ""
        self.debug = set(debug)
        self.stop_after = stop_after
        self.nc = bass.Bass("TRN2", target_bir_lowering=False)
        self.S = Sched(self.nc)
        self._stack = None
        self.outs = []
        self.off = 16512
        self.hw = 0

    def din(self, name, shape, dt=F32):
        return self.nc.dram_tensor(name, list(shape), dt, kind="ExternalInput").ap()

    def dscratch(self, name, shape, dt):
        kind = "ExternalOutput" if name in self.debug else "Internal"
        if kind == "ExternalOutput":
            self.outs.append(name)
        return self.nc.dram_tensor(name, list(shape), dt, kind=kind).ap()

    def sb(self, name, shape, dt):
        isz = 4 if dt == F32 else 2
        n = 1
        for d_ in shape[1:]:
            n *= d_
        nbytes = (n * isz + 31) // 32 * 32
        off = self.off
        assert off + nbytes <= 229344, ("SBUF overflow", name, off, nbytes)
        self.off += nbytes
        self.hw = max(self.hw, self.off)
        return self.nc.alloc_sbuf_tensor_at(name, list(shape), dt, offset=off)

    def psum(self, name, shape, dt=F32):
        return self._stack.enter_context(self.nc.psum_tensor(name, list(shape), dt))

    def build(self):
        import contextlib
        with contextlib.ExitStack() as st:
            self._stack = st
            self._program()
            self.S.emit()
        return self.nc

    def _program(self):
        nc, S = self.nc, self.S
        op = S.op
        x = self.din("x", [NTOK, D])
        c_in = self.din("c", [1, D])
        w_ada = self.din("w_ada", [D, 9 * D])
        b_ada = self.din("b_ada", [1, 9 * D])
        g_pre = self.din("g_pre", [1, 3 * D])
        g_post = self.din("g_post", [1, 3 * D])
        w_gu = [self.din("w_gu1", [D, 2 * DFF]), self.din("w_gu2", [D, 2 * DFF])]
        w_dn = [self.din("w_dn1", [DFF, D]), self.din("w_dn2", [DFF, D])]
        w_in = self.din("w_in", [D, W_IN_COLS])
        w_a2 = self.din("w_a2", [16, 512])
        b_a = self.din("b_a", [1, 512])
        b_f = self.din("b_f", [1, 16])
        g_gla = self.din("g_gla", [1, D])
        w_pa = self.din("w_pa", [D, D])
        w_pb = self.din("w_pb", [D, D])
        w_out = self.din("w_out", [D, D])
        consts = self.din("consts", [128, 512])
        meta = self.din("meta", [128, 8])
        out = self.nc.dram_tensor("out", [NTOK, D], F32, kind="ExternalOutput").ap()
        x1d = self.dscratch("x1d", [NTOK, D], F32)
        x2d = self.dscratch("x2d", [NTOK, D], F32)

        cst = self.sb("cst", [128, 512], F32)
        ident_f = cst[:, 0:128]
        U_f = cst[:, 128:256]
        ones_f = cst[:, 256:384]
        cstb = self.sb("cstb", [128, 384], BF16)
        ident_b = cstb[:, 0:128]
        U_b = cstb[:, 128:256]
        ones_b = cstb[:, 256:384]
        metat = self.sb("metat", [128, 8], F32)
        vstage = self.sb("vstage", [128, 128], F32)
        modc = self.sb("modc", [128, 72], F32)
        gcol = self.sb("gcol", [128, 48], F32)
        scale_c = self.sb("scale_c", [128, 24], F32)
        gate_c = self.sb("gate_c", [128, 24], F32)
        gp_row = [self.sb("gp_row%d" % i, [128, D], F32) for i in range(3)]
        stat = self.sb("stat", [128, 96], F32)
        sc_col = self.sb("sc_col", [128, 8], BF16)

        ps = [self.psum("ps%d" % i, [128, 512], F32) for i in range(8)]
        psk = ["ps%d" % i for i in range(8)]

        h2T = self.sb("h2T", [128, KC, NTOK], BF16)
        mark = self.off

        gen = [0]

        def alloc_ffn():
            gen[0] += 1
            g = "_%d" % gen[0]
            self.off = mark
            B = {}
            B["hT"] = self.sb("hT" + g, [128, KC, 1024], BF16)
            B["actT"] = self.sb("actT" + g, [128, HC, 1024], BF16)
            B["wdn"] = self.sb("wdn" + g, [128, HC, D], BF16)
            B["wgu"] = [self.sb("wgu%d" % i + g, [128, KC, 256], BF16) for i in range(3)]
            B["xs"] = [self.sb("xs%d" % i + g, [128, D], F32) for i in range(3)]
            B["xn"] = [self.sb("xn%d" % i + g, [128, D], BF16) for i in range(2)]
            B["junk"] = self.sb("junk" + g, [128, D], BF16)
            B["tmp"] = [self.sb("tmp%d" % i + g, [128, D], F32) for i in range(2)]
            B["xo"] = [self.sb("xo%d" % i + g, [128, D], F32) for i in range(2)]
            B["sg"] = [self.sb("sg%d" % i + g, [128, 512], F32) for i in range(2)]
            return B

        B = alloc_ffn()
        actT, tmp = B["actT"], B["tmp"]

        op("sp", lambda e: e.dma_start(out=cst[:, :], in_=consts), w=["cst"], dma="cst")
        op("sp", lambda e: e.dma_start(out=metat[:, :], in_=meta), w=["meta"], dma="meta")
        op("dve", lambda e: e.tensor_copy(out=cstb[:, :], in_=cst[:, 0:384]), r=["cst"], w=["cstb"])
        op("pool", lambda e: e.memset(stat[:, 63:64], EPS), w=["epsc"])
        op("pool", lambda e: e.memset(stat[:, 62:63], 1.0), w=["onec"])
        vcnt = [0]

        def vec_to_cols(vec_ap, n, dst, wkey, bank, func=None):
            vcnt[0] += 1
            op("sp", lambda e: e.dma_start(out=vstage[0:n, :], in_=vec_ap.rearrange("o (j p) -> (o j) p", p=128)),
               w=["vstage"], dma="vstage")
            op("pe", lambda e: e.transpose(out=ps[bank][:, 0:n], in_=vstage[0:n, :], identity=ident_f[0:n, 0:n]),
               r=["vstage", "cst"], w=[psk[bank]])
            if func is None:
                op("dve", lambda e: e.tensor_copy(out=dst, in_=ps[bank][:, 0:n]), r=[psk[bank]], w=[wkey])
            else:
                op("act", lambda e: e.activation(out=dst, in_=ps[bank][:, 0:n], func=func), r=[psk[bank]], w=[wkey])

        vec_to_cols(c_in, 8, sc_col[:, :], "sc_col", 0, func=AF.Silu)
        vec_to_cols(g_pre, 24, gcol[:, 0:24], "gcol_a", 1)
        vec_to_cols(g_post, 24, gcol[:, 24:48], "gcol_b", 0)

        wada_v = w_ada.rearrange("(kc p) n -> p kc n", p=128)
        wa = [actT[:, 0:8, :], actT[:, 8:16, :]]
        wak = ["wa0", "wa1"]
        for i in range(9):
            sl = i % 2
            op("pool", lambda e, i=i, sl=sl: e.dma_start(out=wa[sl], in_=wada_v[:, :, i * D:(i + 1) * D]),
               w=[wak[sl]], dma=wak[sl])
            for m in range(8):
                for kc in range(KC):
                    op("pe", lambda e, i=i, m=m, kc=kc, sl=sl: e.matmul(
                        ps[2][:, i * 8 + m:i * 8 + m + 1], lhsT=wa[sl][:, kc, m * 128:(m + 1) * 128],
                        rhs=sc_col[:, kc:kc + 1], start=(kc == 0), stop=(kc == KC - 1)),
                       r=[wak[sl], "sc_col"], w=[psk[2]])
        vec_to_cols(b_ada, 72, modc[:, :], "modc_b", 3)
        op("dve", lambda e: e.tensor_tensor(out=modc[:, :], in0=modc[:, :], in1=ps[2][:, 0:72], op=ALU.add),
           r=["modc_b", psk[2]], w=["modc"])
        for i in range(3):
            sc = modc[:, (3 * i + 1) * 8:(3 * i + 2) * 8]
            gt = modc[:, (3 * i + 2) * 8:(3 * i + 3) * 8]
            op("dve", lambda e, i=i, sc=sc: e.scalar_tensor_tensor(
                out=scale_c[:, i * 8:(i + 1) * 8], in0=sc, scalar=1.0, in1=gcol[:, i * 8:(i + 1) * 8],
                op0=ALU.add, op1=ALU.mult), r=["modc", "gcol_a", "gcol_b"], w=["scale_c"])
            fac = 1.0 if i == 1 else 0.5
            op("dve", lambda e, i=i, gt=gt, fac=fac: e.scalar_tensor_tensor(
                out=gate_c[:, i * 8:(i + 1) * 8], in0=gt, scalar=fac, in1=gcol[:, 24 + i * 8:24 + (i + 1) * 8],
                op0=ALU.mult, op1=ALU.mult), r=["modc", "gcol_a", "gcol_b"], w=["gate_c"])
        for i in range(3):
            for m in range(8):
                op("dve", lambda e, i=i, m=m: e.tensor_scalar(
                    out=tmp[0][:, m * 128:(m + 1) * 128], in0=ident_f, scalar1=gate_c[:, i * 8 + m:i * 8 + m + 1],
                    scalar2=None, op0=ALU.mult), r=["gate_c", "cst"], w=["tmp0"])
            for hfi in range(2):
                for m in range(4):
                    mm = hfi * 4 + m
                    op("pe", lambda e, hfi=hfi, m=m, mm=mm: e.matmul(
                        ps[4 + hfi][:, m * 128:(m + 1) * 128], lhsT=ones_f, rhs=tmp[0][:, mm * 128:(mm + 1) * 128],
                        start=True, stop=True), r=["tmp0", "cst"], w=[psk[4 + hfi]])
                op("act", lambda e, i=i, hfi=hfi: e.copy(out=gp_row[i][:, hfi * 512:(hfi + 1) * 512], in_=ps[4 + hfi][:, :]),
                   r=[psk[4 + hfi]], w=["gp_row%d" % i])
        if self.stop_after == "mods":
            dbg = self.nc.dram_tensor("dbg_mods", [128, 72 + 24 + 24 + 1024], F32, kind="ExternalOutput").ap()
            self.outs.append("dbg_mods")
            op("sp", lambda e: e.dma_start(out=dbg[:, 0:72], in_=modc[:, :]), r=["modc"], dma="dbg")
            op("sp", lambda e: e.dma_start(out=dbg[:, 72:96], in_=scale_c[:, :]), r=["scale_c"], dma="dbg")
            op("sp", lambda e: e.dma_start(out=dbg[:, 96:120], in_=gate_c[:, :]), r=["gate_c"], dma="dbg")
            op("sp", lambda e: e.dma_start(out=dbg[:, 120:120 + 1024], in_=gp_row[1][:, :]), r=["gp_row1"], dma="dbg")
            return
        S.barrier()

        cnt = {"xs": 0, "xn": 0, "st": 0, "pst": 0}

        def rstd_from(ssq_aps, dst, rkeys, wkey):
            if len(ssq_aps) == 2:
                op("dve", lambda e: e.tensor_tensor(out=dst, in0=ssq_aps[0], in1=ssq_aps[1], op=ALU.add),
                   r=rkeys, w=[wkey])
                src = dst
                rk = [wkey]
            else:
                src = ssq_aps[0]
                rk = rkeys
            op("act", lambda e: e.activation(out=dst, in_=src, func=AF.Sqrt, scale=1.0 / D, bias=stat[:, 63:64]),
               r=rk + ["epsc"], w=[wkey])
            op("dve", lambda e: e.reciprocal(out=dst, in_=dst), r=[wkey], w=[wkey])

        def prenorm_T(xt, xkey, sub, dstT, col0, dkey, B):
            junk, xn = B["junk"], B["xn"]
            k = cnt["st"] % 16
            cnt["st"] += 1
            ssq = stat[:, 16 + k:17 + k]
            rs = stat[:, 32 + k:33 + k]
            sk, rk = "ssq%d" % k, "rs%d" % k
            op("pool", lambda e: e.memset(ssq, 0.0), w=[sk])
            op("act", lambda e: e.activation(out=junk[:, :], in_=xt, func=AF.Square, accum_out=ssq),
               r=[xkey, sk], w=[sk, "junk"])
            rstd_from([ssq], rs, [sk], rk)
            b = cnt["xn"] % 2
            cnt["xn"] += 1
            op("act", lambda e: e.activation(out=xn[b][:, :], in_=xt, func=AF.Copy, scale=rs),
               r=[xkey, rk], w=["xn%d" % b])
            pb = 2 + (cnt["pst"] % 2)
            cnt["pst"] += 1
            pst = ps[pb][:, 0:512].bitcast(BF16)
            for kc in range(KC):
                op("pe", lambda e, kc=kc: e.transpose(out=pst[:, kc * 128:(kc + 1) * 128],
                                                      in_=xn[b][:, kc * 128:(kc + 1) * 128], identity=ident_b),
                   r=["xn%d" % b, "cstb"], w=[psk[pb]])
            for kc in range(KC):
                eng = "dve" if kc % 2 == 0 else "pool"
                if eng == "pool":
                    eng = "dve"
                op(eng, lambda e, kc=kc: e.tensor_scalar(
                    out=dstT[:, kc, col0:col0 + 128], in0=pst[:, kc * 128:(kc + 1) * 128],
                    scalar1=scale_c[:, sub * 8 + kc:sub * 8 + kc + 1],
                    scalar2=modc[:, (3 * sub) * 8 + kc:(3 * sub) * 8 + kc + 1],
                    op0=ALU.mult, op1=ALU.add), r=[psk[pb], "scale_c", "modc"], w=[dkey])

        def post_res(pa, pb_, sub, ti, x_src, x_dst, junk, tmpb, tmpk, xs, xob, xok, post_cb):
            k = cnt["st"] % 16
            cnt["st"] += 1
            sa, sb_, rs = stat[:, 16 + k:17 + k], stat[:, 64 + k:65 + k], stat[:, 32 + k:33 + k]
            ska, skb, rk = "ssq%d" % k, "ssqb%d" % k, "rs%d" % k
            op("pool", lambda e: e.memset(sa, 0.0), w=[ska])
            op("pool", lambda e: e.memset(sb_, 0.0), w=[skb])
            op("act", lambda e: e.activation(out=junk[:, 0:512], in_=ps[pa][:, :], func=AF.Square, accum_out=sa),
               r=[psk[pa], ska], w=[ska, "junk"])
            op("act", lambda e: e.activation(out=junk[:, 512:1024], in_=ps[pb_][:, :], func=AF.Square, accum_out=sb_),
               r=[psk[pb_], skb], w=[skb, "junk"])
            rstd_from([sa, sb_], rs, [ska, skb], rk)
            for half, pbank in ((0, pa), (1, pb_)):
                op("dve", lambda e, half=half, pbank=pbank: e.scalar_tensor_tensor(
                    out=tmpb[:, half * 512:(half + 1) * 512], in0=ps[pbank][:, :], scalar=rs,
                    in1=gp_row[sub][:, half * 512:(half + 1) * 512], op0=ALU.mult, op1=ALU.mult),
                   r=[psk[pbank], rk, "gp_row%d" % sub], w=[tmpk])
            s_ = cnt["xs"] % len(xs)
            cnt["xs"] += 1
            xsk = "xs%d" % s_
            op("sp", lambda e: e.dma_start(out=xs[s_][:, :], in_=x_src[ti * 128:(ti + 1) * 128, :]),
               r=["dram_x%d_%d" % (sub, ti)], w=[xsk], dma=xsk)
            op("pool", lambda e: e.tensor_tensor(out=xob[:, :], in0=xs[s_][:, :], in1=tmpb[:, :], op=ALU.add),
               r=[xsk, tmpk], w=[xok])
            op("sp", lambda e: e.dma_start(out=x_dst[ti * 128:(ti + 1) * 128, :], in_=xob[:, :]),
               r=[xok], w=["dram_x%d_%d" % (sub + 1, ti)], dma=xok)
            if post_cb is not None:
                post_cb(xob[:, :], xok, ti)

        def ffn(sub, fi, x_src, x_dst, post_cb, B):
            hT, actT, wdn, wgu, xs, junk, tmp, xo, sg = (B[k] for k in ("hT", "actT", "wdn", "wgu", "xs", "junk", "tmp", "xo", "sg"))
            wgu_v = w_gu[fi].rearrange("(kc p) n -> p kc n", p=128)
            wdn_v = w_dn[fi].rearrange("(c p) n -> p c n", p=128)
            for hf in range(2):
                if x_src is not None:
                    for i in range(8):
                        ti = hf * 8 + i
                        s = cnt["xs"] % 3
                        cnt["xs"] += 1
                        op("sp", lambda e, s=s, ti=ti: e.dma_start(out=xs[s][:, :], in_=x_src[ti * 128:(ti + 1) * 128, :]),
                           r=["dram_x%d_%d" % (sub, ti)], w=["xs%d" % s], dma="xs%d" % s)
                        prenorm_T(xs[s][:, :], "xs%d" % s, sub, hT, i * 128, "hT_%d" % i, B)
                if hf == 0:
                    for c in range(HC):
                        op("pool", lambda e, c=c: e.dma_start(out=wdn[:, c, :], in_=wdn_v[:, c, :]),
                           w=["wdn_%d" % c], dma="wdn%d" % (c % 4))
                for c in range(HC):
                    sl = (hf * HC + c) % 3
                    op("pool", lambda e, c=c, sl=sl: e.dma_start(out=wgu[sl][:, :, 0:128], in_=wgu_v[:, :, c * 128:(c + 1) * 128]),
                       w=["wgu%da" % sl], dma="wgu%da" % sl)
                    op("pool", lambda e, c=c, sl=sl: e.dma_start(out=wgu[sl][:, :, 128:256],
                                                                 in_=wgu_v[:, :, DFF + c * 128:DFF + (c + 1) * 128]),
                       w=["wgu%db" % sl], dma="wgu%db" % sl)
                    for tg in range(2):
                        bg, bu = 2 * tg, 2 * tg + 1
                        hk = ["hT_%d" % i for i in range(tg * 4, tg * 4 + 4)]
                        for kc in range(KC):
                            op("pe", lambda e, kc=kc, sl=sl, tg=tg, bg=bg: e.matmul(
                                ps[bg][:, :], lhsT=wgu[sl][:, kc, 0:128], rhs=hT[:, kc, tg * 512:(tg + 1) * 512],
                                start=(kc == 0), stop=(kc == KC - 1)), r=["wgu%da" % sl] + hk, w=[psk[bg]])
                        for kc in range(KC):
                            op("pe", lambda e, kc=kc, sl=sl, tg=tg, bu=bu: e.matmul(
                                ps[bu][:, :], lhsT=wgu[sl][:, kc, 128:256], rhs=hT[:, kc, tg * 512:(tg + 1) * 512],
                                start=(kc == 0), stop=(kc == KC - 1)), r=["wgu%db" % sl] + hk, w=[psk[bu]])
                        op("act", lambda e, tg=tg, bg=bg: e.activation(out=sg[tg][:, :], in_=ps[bg][:, :], func=AF.Silu),
                           r=[psk[bg]], w=["sg%d" % tg])
                        op("dve", lambda e, tg=tg, bu=bu, c=c: e.tensor_tensor(
                            out=actT[:, c, tg * 512:(tg + 1) * 512], in0=sg[tg][:, :], in1=ps[bu][:, :], op=ALU.mult),
                           r=["sg%d" % tg, psk[bu]], w=["actT_%d_%d" % (c, tg)])
                pending = None
                for i in range(8):
                    ti = hf * 8 + i
                    tg = i // 4
                    pa, pb_ = (4, 5) if i % 2 == 0 else (6, 7)
                    for half, pbank in ((0, pa), (1, pb_)):
                        for c in range(HC):
                            op("pe", lambda e, c=c, i=i, half=half, pbank=pbank: e.matmul(
                                ps[pbank][:, :], lhsT=actT[:, c, i * 128:(i + 1) * 128],
                                rhs=wdn[:, c, half * 512:(half + 1) * 512], start=(c == 0), stop=(c == HC - 1)),
                               r=["actT_%d_%d" % (c, tg), "wdn_%d" % c], w=[psk[pbank]])
                    if pending is not None:
                        post_cb(*pending)
                        pending = None
                    post_res(pa, pb_, sub, ti, x_src, x_dst, junk, tmp[i % 2], "tmp%d" % (i % 2), xs, xo[i % 2], "xo%d" % (i % 2), None)
                    if post_cb is not None:
                        pending = (xo[i % 2][:, :], "xo%d" % (i % 2), ti)
                if pending is not None:
                    post_cb(*pending)
                    pending = None

        def post1(xt, xkey, ti):
            prenorm_T(xt, xkey, 1, h2T, ti * 128, "h2T_%d" % ti, B)

        S.muted = "ffn1" in self.skip
        ffn(0, 0, x, x1d, post1, B)
        S.muted = False
        if "ffn1" in self.skip:
            op("pool", lambda e: e.memset(h2T[:, :, :], 0.01), w=["h2T_%d" % i for i in range(NT)])
        if self.stop_after == "ffn1":
            dbg = self.nc.dram_tensor("dbg_h2T", [128, KC * NTOK], BF16, kind="ExternalOutput").ap()
            self.outs.append("dbg_h2T")
            op("sp", lambda e: e.dma_start(out=dbg, in_=h2T[:, :, :].rearrange("p k n -> p (k n)")),
               r=["h2T_%d" % i for i in range(NT)], dma="dbg")
            return
        S.barrier()

        RG = [[0, 1, 2, 3], [4, 5, 6, 7]]
        kT_src = self.dscratch("kT_src", [8 * 128, NTOK], BF16)
        kT_g = [self.dscratch("kT_g%d" % q, [4 * 256, NTOK], BF16) for q in range(4)]
        v_src = self.dscratch("v_src", [NTOK, 1024], BF16)
        v_g = [self.dscratch("v_g%d" % q, [4 * 512, 1024], BF16) for q in range(4)]
        fl_src = self.dscratch("fl_src", [128, 256], F32)
        fl_g = self.dscratch("fl_g", [4 * 128, 256], F32)
        gl_src = self.dscratch("gl_src", [128, 1536], F32)
        gl_g = self.dscratch("gl_g", [4 * 128, 1536], F32)
        win_v = w_in.rearrange("(kc p) n -> p kc n", p=128)
        h2k = lambda t0, t1: ["h2T_%d" % i for i in range(t0, t1)]
        onec = stat[:, 62:63]

        S.muted = "fox" in self.skip
        self.off = mark
        obT = self.sb("obT", [128, 8, NTOK], BF16)
        lf = self.sb("lf", [128, 256], F32)
        lfa = self.sb("lfa", [128, 256], F32)
        floc = self.sb("floc", [128, 256], F32)
        cq = self.sb("cq", [128, 64], F32)
        rall = self.sb("rall", [128, 256], F32)
        bfrow = self.sb("bfrow", [1, 16], F32)
        bfrep = self.sb("bfrep", [128, 16], F32)
        oaT = self.sb("oaT", [128, 8, NTOK], BF16)
        mark2 = self.off
        wk = [self.sb("wk%d" % i, [128, KC, 128], BF16) for i in range(2)]
        wv = self.sb("wv", [128, KC, 1024], BF16)
        wf = self.sb("wf", [128, KC, 16], BF16)
        ksb = [self.sb("ksb%d" % i, [128, NTOK], BF16) for i in range(2)]
        vsb = [self.sb("vsb%d" % i, [128, 1024], BF16) for i in range(2)]

        op("sp", lambda e: e.dma_start(out=bfrow[:, :], in_=b_f), w=["bfrow"], dma="bfrow")
        op("pe", lambda e: e.matmul(ps[5][:, 0:16], lhsT=ones_f[0:1, :], rhs=bfrow[0:1, :], start=True, stop=True),
           r=["bfrow", "cst"], w=[psk[5]])
        op("dve", lambda e: e.tensor_copy(out=bfrep[:, :], in_=ps[5][:, 0:16]), r=[psk[5]], w=["bfrep"])
        ev = [0]

        def evac(out_ap, in_ap, rk, wk_):
            ev[0] += 1
            if ev[0] % 2 == 0:
                op("act", lambda e: e.copy(out=out_ap, in_=in_ap), r=rk, w=wk_)
            else:
                op("dve", lambda e: e.tensor_copy(out=out_ap, in_=in_ap), r=rk, w=wk_)

        op("pool", lambda e: e.dma_start(out=wv[:, :, :], in_=win_v[:, :, O_VB:O_VB + 1024]), w=["wv"], dma="wv")
        op("pool", lambda e: e.dma_start(out=wf[:, :, :], in_=win_v[:, :, O_FB:O_FB + 16]), w=["wf"], dma="wf")
        for hp in range(8):
            sl = hp % 2
            op("pool", lambda e, hp=hp, sl=sl: e.dma_start(out=wk[sl][:, :, :], in_=win_v[:, :, O_KB + hp * 128:O_KB + (hp + 1) * 128]),
               w=["wk%d" % sl], dma="wk%d" % sl)
            for tg in range(4):
                bank = tg % 2
                for kc in range(KC):
                    op("pe", lambda e, kc=kc, sl=sl, tg=tg, bank=bank: e.matmul(
                        ps[bank][:, :], lhsT=wk[sl][:, kc, :], rhs=h2T[:, kc, tg * 512:(tg + 1) * 512],
                        start=(kc == 0), stop=(kc == KC - 1)), r=["wk%d" % sl] + h2k(tg * 4, tg * 4 + 4), w=[psk[bank]])
                evac(ksb[sl][:, tg * 512:(tg + 1) * 512], ps[bank][:, :], [psk[bank]], ["ksb%d_%d" % (sl, tg)])
            op("sp", lambda e, hp=hp, sl=sl: e.dma_start(out=kT_src[hp * 128:(hp + 1) * 128, :], in_=ksb[sl][:, :]),
               r=["ksb%d_%d" % (sl, tg) for tg in range(4)], w=["kTsrc_%d" % hp], dma="ksrc%d" % sl)
        for blk in range(16):
            b = blk % 2
            for half in range(2):
                bank = 2 + half
                for kc in range(KC):
                    op("pe", lambda e, kc=kc, blk=blk, half=half, bank=bank: e.matmul(
                        ps[bank][:, :], lhsT=h2T[:, kc, blk * 128:(blk + 1) * 128], rhs=wv[:, kc, half * 512:(half + 1) * 512],
                        start=(kc == 0), stop=(kc == KC - 1)), r=["wv", "h2T_%d" % blk], w=[psk[bank]])
                evac(vsb[b][:, half * 512:(half + 1) * 512], ps[bank][:, :], [psk[bank]], ["vsb%d_%d" % (b, half)])
            for kc in range(KC):
                op("pe", lambda e, kc=kc, blk=blk: e.matmul(
                    ps[4][:, blk * 16:(blk + 1) * 16], lhsT=h2T[:, kc, blk * 128:(blk + 1) * 128], rhs=wf[:, kc, :],
                    start=(kc == 0), stop=(kc == KC - 1)), r=["wf", "h2T_%d" % blk], w=[psk[4]])
            op("sp", lambda e, blk=blk, b=b: e.dma_start(out=v_src[blk * 128:(blk + 1) * 128, :], in_=vsb[b][:, :]),
               r=["vsb%d_0" % b, "vsb%d_1" % b], w=["vsrc_%d" % blk], dma="vsrc%d" % b)
        for blk in range(16):
            op("dve", lambda e, blk=blk: e.tensor_tensor(out=lf[:, blk * 16:(blk + 1) * 16], in0=ps[4][:, blk * 16:(blk + 1) * 16],
                                                        in1=bfrep[:, :], op=ALU.add), r=[psk[4], "bfrep"], w=["lf"])
        op("act", lambda e: e.activation(out=lfa[:, :], in_=lf[:, :], func=AF.Abs), r=["lf"], w=["lfa"])
        op("act", lambda e: e.activation(out=lfa[:, :], in_=lfa[:, :], func=AF.Exp, scale=-1.0), r=["lfa"], w=["lfa"])
        op("act", lambda e: e.activation(out=lfa[:, :], in_=lfa[:, :], func=AF.Ln, bias=onec, scale=1.0), r=["lfa", "onec"], w=["lfa"])
        op("dve", lambda e: e.tensor_single_scalar(out=lf[:, :], in_=lf[:, :], scalar=0.0, op=ALU.min), r=["lf"], w=["lf"])
        op("dve", lambda e: e.tensor_tensor(out=lf[:, :], in0=lf[:, :], in1=lfa[:, :], op=ALU.subtract), r=["lf", "lfa"], w=["lf"])
        for blk in range(16):
            lst = [(ones_f, b2) for b2 in range(blk)] + [(U_f, blk)]
            for i, (lh, b2) in enumerate(lst):
                op("pe", lambda e, lh=lh, b2=b2, blk=blk, i=i, n=len(lst): e.matmul(
                    ps[5][:, blk * 16:(blk + 1) * 16], lhsT=lh, rhs=lf[:, b2 * 16:(b2 + 1) * 16],
                    start=(i == 0), stop=(i == n - 1)), r=["lf", "cst"], w=[psk[5]])
        op("dve", lambda e: e.tensor_copy(out=floc[:, :], in_=ps[5][:, 0:256]), r=[psk[5]], w=["floc"])
        op("sp", lambda e: e.dma_start(out=fl_src, in_=floc[:, :]), r=["floc"], w=["flsrc"], dma="flsrc")
        op("pool", lambda e: e.memset(cq[:, 0:16], 0.0), w=["cq0"])
        for qt in range(1, 4):
            for i in range(4 * qt):
                op("pe", lambda e, qt=qt, i=i: e.matmul(ps[6][:, qt * 16:(qt + 1) * 16], lhsT=ones_f, rhs=lf[:, i * 16:(i + 1) * 16],
                                                       start=(i == 0), stop=(i == 4 * qt - 1)), r=["lf", "cst"], w=[psk[6]])
        op("dve", lambda e: e.tensor_copy(out=cq[:, 16:64], in_=ps[6][:, 16:64]), r=[psk[6]], w=["cq1"])
        for blk in range(16):
            q_ = blk // 4
            op("dve", lambda e, blk=blk, q_=q_: e.tensor_tensor(out=rall[:, blk * 16:(blk + 1) * 16], in0=floc[:, blk * 16:(blk + 1) * 16],
                                                               in1=cq[:, q_ * 16:(q_ + 1) * 16], op=ALU.subtract),
               r=["floc", "cq0", "cq1"], w=["rall"])
        if self.stop_after == "m1a":
            dbg = self.nc.dram_tensor("dbg_lf", [128, 768], F32, kind="ExternalOutput").ap()
            self.outs.append("dbg_lf")
            op("sp", lambda e: e.dma_start(out=dbg[:, 0:256], in_=lf[:, :]), r=["lf"], dma="dbg")
            op("sp", lambda e: e.dma_start(out=dbg[:, 256:512], in_=floc[:, :]), r=["floc"], dma="dbg")
            op("sp", lambda e: e.dma_start(out=dbg[:, 512:768], in_=rall[:, :]), r=["rall"], dma="dbg")
            return
        for q in range(4):
            op("pool", lambda e, q=q: e.collective_compute("AllGather", ALU.bypass, replica_groups=RG,
                                                           ins=[kT_src[q * 256:(q + 1) * 256, :]], outs=[kT_g[q]]),
               r=["kTsrc_%d" % i for i in (2 * q, 2 * q + 1)], w=["kTg%d" % q], dma="ag_k%d" % q, inc=1)
        for q in range(4):
            op("pool", lambda e, q=q: e.collective_compute("AllGather", ALU.bypass, replica_groups=RG,
                                                           ins=[v_src[q * 512:(q + 1) * 512, :]], outs=[v_g[q]]),
               r=["vsrc_%d" % i for i in range(16)], w=["vg%d" % q], dma="ag_v%d" % q, inc=1)
        op("pool", lambda e: e.collective_compute("AllGather", ALU.bypass, replica_groups=RG, ins=[fl_src], outs=[fl_g]),
           r=["flsrc"], w=["flg_d"], dma="ag_f", inc=1)
        if self.stop_after == "m1":
            dbg = self.nc.dram_tensor("dbg_lf", [128, 768], F32, kind="ExternalOutput").ap()
            self.outs.append("dbg_lf")
            op("sp", lambda e: e.dma_start(out=dbg[:, 0:256], in_=lf[:, :]), r=["lf"], dma="dbg")
            op("sp", lambda e: e.dma_start(out=dbg[:, 256:512], in_=floc[:, :]), r=["floc"], dma="dbg")
            op("sp", lambda e: e.dma_start(out=dbg[:, 512:768], in_=rall[:, :]), r=["rall"], dma="dbg")
            return
        S.barrier(skip_prefix="ag_")

        S.muted = False
        self.off = mark2
        mark3 = self.off
        wal = self.sb("wal", [128, KC, 16], BF16)
        alT = self.sb("alT", [128, NTOK], F32)
        wa2 = self.sb("wa2", [128, 512], F32)
        bac = self.sb("bac", [128, 4], F32)
        ggrow1 = self.sb("ggrow1", [1, D], F32)
        ggrow = self.sb("ggrow", [128, D], F32)
        wqa = self.sb("wqa", [128, KC, 128], BF16)
        wka = self.sb("wka", [128, KC, 128], BF16)
        wva = self.sb("wva", [128, KC, 256], BF16)
        wra = self.sb("wra", [128, KC, 256], BF16)
        qT = self.sb("qT", [128, NTOK], BF16)
        kT = self.sb("kT", [128, NTOK], BF16)
        va = self.sb("va", [128, 16, 256], BF16)
        bT = self.sb("bT", [128, NTOK], F32)
        tA = self.sb("tA", [128, 512], F32)
        tM = self.sb("tM", [128, 512], F32)
        Sf = self.sb("Sf", [128, 256], F32)
        Sst = self.sb("Sst", [128, 4, 256], F32)
        NB = 3
        Sb = [self.sb("Sb%d" % i, [128, 256], BF16) for i in range(NB)]
        e1 = [self.sb("e1_%d" % i, [128, 128], F32) for i in range(NB)]
        e2 = [self.sb("e2_%d" % i, [128, 128], F32) for i in range(NB)]
        kht = [self.sb("kht%d" % i, [128, 128], BF16) for i in range(NB)]
        kh_all = self.sb("kh_all", [128, 16, 128], BF16)
        At_all = self.sb("At_all", [128, 16, 128], BF16)
        dcolT = self.sb("dcolT", [128, 16], F32)
        sr = [self.sb("sr%d" % i, [128, 256], F32) for i in range(2)]
        osb = [self.sb("osb%d" % i, [128, 256], F32) for i in range(2)]
        oab = [self.sb("oab%d" % i, [128, 256], BF16) for i in range(2)]
        gjunk = self.sb("gjunk", [128, 256], BF16)
        gsq = self.sb("gsq", [128, 48], F32)
        glsrc = self.sb("glsrc", [128, 1536], F32)
        glg = glsrc[:, :]
        gst = self.sb("gst", [128, 8], F32)

        op("pool", lambda e: e.dma_start(out=wal[:, :, :], in_=win_v[:, :, O_AL:O_AL + 16]), w=["wal"], dma="wal")
        op("pool", lambda e: e.memset(wa2[:, :], 0.0), w=["wa2"])
        op("pool", lambda e: e.memset(alT[:, :], 0.0), w=["alT"])
        op("sp", lambda e: e.dma_start(out=wa2[0:16, :], in_=w_a2), w=["wa2"], dma="wa2")
        op("sp", lambda e: e.dma_start(out=ggrow1[:, :], in_=g_gla), w=["ggrow1"], dma="ggrow1")
        vec_to_cols(b_a, 4, bac[:, :], "bac", 7)
        for half in range(2):
            op("pe", lambda e, half=half: e.matmul(ps[half][:, :], lhsT=ones_f[0:1, :], rhs=ggrow1[0:1, half * 512:(half + 1) * 512],
                                                  start=True, stop=True), r=["ggrow1", "cst"], w=[psk[half]])
            op("dve", lambda e, half=half: e.tensor_copy(out=ggrow[:, half * 512:(half + 1) * 512], in_=ps[half][:, :]),
               r=[psk[half]], w=["ggrow"])
        for tg in range(4):
            for kc in range(KC):
                op("pe", lambda e, kc=kc, tg=tg: e.matmul(ps[2][0:16, :], lhsT=wal[:, kc, :], rhs=h2T[:, kc, tg * 512:(tg + 1) * 512],
                                                         start=(kc == 0), stop=(kc == KC - 1)), r=["wal"] + h2k(tg * 4, tg * 4 + 4), w=[psk[2]])
            op("dve", lambda e, tg=tg: e.tensor_copy(out=alT[0:16, tg * 512:(tg + 1) * 512], in_=ps[2][0:16, :]), r=[psk[2]], w=["alT"])

        if self.stop_after == "glaP":
            dbg = self.nc.dram_tensor("dbg_sst", [128, 1024], F32, kind="ExternalOutput").ap()
            self.outs.append("dbg_sst")
            op("sp", lambda e: e.dma_start(out=dbg, in_=ggrow[:, :]), r=["ggrow", "alT", "bac"], dma="dbg")
            return

        def gla_head(hh, full, part="both"):
            S.muted = (part == "loop")
            op("pool", lambda e: e.dma_start(out=wka[:, :, :], in_=win_v[:, :, O_KA + hh * 128:O_KA + (hh + 1) * 128]), w=["wka"], dma="wka")
            op("pool", lambda e: e.dma_start(out=wva[:, :, :], in_=win_v[:, :, O_VA + hh * 256:O_VA + (hh + 1) * 256]), w=["wva"], dma="wva")
            if full:
                op("pool", lambda e: e.dma_start(out=wqa[:, :, :], in_=win_v[:, :, O_QA + hh * 128:O_QA + (hh + 1) * 128]), w=["wqa"], dma="wqa")
                op("pool", lambda e: e.dma_start(out=wra[:, :, :], in_=win_v[:, :, O_RA + hh * 256:O_RA + (hh + 1) * 256]), w=["wra"], dma="wra")
            bcol = bac[:, hh:hh + 1]
            def pre(c):
                b = c % NB
                cs = slice(c * 128, (c + 1) * 128)
                tg = c // 4
                op("act", lambda e, b=b, cs=cs: e.activation(out=e2[b][:, :], in_=bT[:, cs], func=AF.Exp, scale=-1.0 / 16), r=["bT_%d" % c], w=["e2_%d" % b])
                op("act", lambda e, b=b, cs=cs: e.activation(out=e1[b][:, :], in_=bT[:, cs], func=AF.Exp, scale=1.0 / 16), r=["bT_%d" % c], w=["e1_%d" % b])
                op("dve", lambda e, b=b, cs=cs: e.tensor_tensor(out=kT[:, cs], in0=kT[:, cs], in1=e2[b][:, :], op=ALU.mult),
                   r=["kT_%d" % tg, "e2_%d" % b], w=["kTc_%d" % c])
                op("dve", lambda e, b=b, cs=cs: e.tensor_scalar(out=kht[b][:, :], in0=kT[:, cs], scalar1=e1[b][:, 127:128], scalar2=None, op0=ALU.mult),
                   r=["kTc_%d" % c, "e1_%d" % b], w=["kht%d" % b])
                op("dve", lambda e, b=b, c=c: e.tensor_copy(out=dcolT[:, c:c + 1], in_=e1[b][:, 127:128]), r=["e1_%d" % b], w=["dcol_%d" % c])
                p3 = ps[3][:, 0:64].bitcast(BF16)
                op("pe", lambda e, b=b, p3=p3: e.transpose(out=p3, in_=kht[b][:, :], identity=ident_b), r=["kht%d" % b, "cstb"], w=[psk[3]])
                op("dve", lambda e, c=c, p3=p3: e.tensor_copy(out=kh_all[:, c, :], in_=p3), r=[psk[3]], w=["kh_%d" % c])
                if full:
                    op("pool", lambda e, b=b, cs=cs: e.tensor_tensor(out=qT[:, cs], in0=qT[:, cs], in1=e1[b][:, :], op=ALU.mult),
                       r=["qT_%d" % tg, "e1_%d" % b], w=["qTc_%d" % c])
                    op("pe", lambda e, cs=cs: e.matmul(ps[4][:, 0:128], lhsT=kT[:, cs], rhs=qT[:, cs], start=True, stop=True),
                       r=["kTc_%d" % c, "qTc_%d" % c], w=[psk[4]])
                    op("dve", lambda e, c=c: e.tensor_tensor(out=At_all[:, c, :], in0=ps[4][:, 0:128], in1=U_f, op=ALU.mult), r=[psk[4], "cst"], w=["At_%d" % c])

            for tg in range(4):
                cs = slice(tg * 512, (tg + 1) * 512)
                op("pe", lambda e, cs=cs: e.matmul(ps[0][:, :], lhsT=wa2[:, hh * 128:(hh + 1) * 128], rhs=alT[:, cs], start=True, stop=True),
                   r=["wa2", "alT"], w=[psk[0]])
                op("act", lambda e: e.activation(out=tM[:, :], in_=ps[0][:, :], func=AF.Identity, bias=bcol, scale=1.0), r=[psk[0], "bac"], w=["tM"])
                op("act", lambda e: e.activation(out=tA[:, :], in_=tM[:, :], func=AF.Abs), r=["tM"], w=["tA"])
                op("act", lambda e: e.activation(out=tA[:, :], in_=tA[:, :], func=AF.Exp, scale=-1.0), r=["tA"], w=["tA"])
                op("act", lambda e: e.activation(out=tA[:, :], in_=tA[:, :], func=AF.Ln, bias=onec, scale=1.0), r=["tA", "onec"], w=["tA"])
                op("dve", lambda e: e.tensor_scalar(out=tM[:, :], in0=tM[:, :], scalar1=0.0, scalar2=None, op0=ALU.min),
                   r=["tM", "tA"], w=["tM"])
                op("dve", lambda e: e.tensor_tensor(out=tA[:, :], in0=tM[:, :], in1=tA[:, :], op=ALU.subtract), r=["tM", "tA"], w=["tA"])
                for j in range(4):
                    c = tg * 4 + j
                    op("dve", lambda e, j=j, c=c: e.tensor_tensor_scan(out=bT[:, c * 128:(c + 1) * 128], data0=ones_f, data1=tA[:, j * 128:(j + 1) * 128],
                                                                    initial=0.0, op0=ALU.mult, op1=ALU.add), r=["tA", "cst"], w=["bT_%d" % c])
                for kc in range(KC):
                    op("pe", lambda e, kc=kc, cs=cs: e.matmul(ps[1][:, :], lhsT=wka[:, kc, :], rhs=h2T[:, kc, cs], start=(kc == 0), stop=(kc == KC - 1)),
                       r=["wka"] + h2k(tg * 4, tg * 4 + 4), w=[psk[1]])
                op("act", lambda e, cs=cs: e.copy(out=kT[:, cs], in_=ps[1][:, :]), r=[psk[1]], w=["kT_%d" % tg] + ["kTc_%d" % (tg * 4 + j_) for j_ in range(4)])
                if full:
                    for kc in range(KC):
                        op("pe", lambda e, kc=kc, cs=cs: e.matmul(ps[1][:, :], lhsT=wqa[:, kc, :], rhs=h2T[:, kc, cs], start=(kc == 0), stop=(kc == KC - 1)),
                           r=["wqa"] + h2k(tg * 4, tg * 4 + 4), w=[psk[1]])
                    op("dve", lambda e, cs=cs: e.tensor_scalar(out=qT[:, cs], in0=ps[1][:, :], scalar1=128.0 ** -0.5, scalar2=None, op0=ALU.mult),
                       r=[psk[1]], w=["qT_%d" % tg] + ["qTc_%d" % (tg * 4 + j_) for j_ in range(4)])
                for j_ in range(4):
                    pre(tg * 4 + j_)
            for blk in range(16):
                for kc in range(KC):
                    op("pe", lambda e, kc=kc, blk=blk: e.matmul(ps[2][:, 0:256], lhsT=h2T[:, kc, blk * 128:(blk + 1) * 128], rhs=wva[:, kc, :],
                                                               start=(kc == 0), stop=(kc == KC - 1)), r=["wva", "h2T_%d" % blk], w=[psk[2]])
                evac(va[:, blk, :], ps[2][:, 0:256], [psk[2]], ["va_%d" % blk])
            S.muted = (part == "loop")
            PRECOMPUTE_MARK = None

            if full:
                op("pool", lambda e: e.memset(gsq[:, :], 0.0), w=["gsq"])

            def kvmm(c):
                kvb = c % 2
                op("pe", lambda e: e.matmul(ps[kvb][:, 0:256], lhsT=kh_all[:, c, :], rhs=va[:, c, :], start=True, stop=True),
                   r=["kh_%d" % c, "va_%d" % c], w=[psk[kvb]])

            def rmm(c):
                rb = 2 + 2 * (c % 3)
                cs = slice(c * 128, (c + 1) * 128)
                for kc in range(KC):
                    op("pe", lambda e, kc=kc: e.matmul(ps[rb][:, 0:256], lhsT=h2T[:, kc, cs], rhs=wra[:, kc, :], start=(kc == 0), stop=(kc == KC - 1)),
                       r=["wra", "h2T_%d" % c], w=[psk[rb]])

            def state(c):
                b = c % NB
                ob = 5 if c % 2 == 0 else 7
                cs = slice(c * 128, (c + 1) * 128)
                if full:
                    op("pool", lambda e: e.tensor_copy(out=Sb[b][:, :], in_=Sf[:, :]), r=["Sf"], w=["Sb%d" % b])
                    op("pe", lambda e: e.matmul(ps[ob][:, 0:256], lhsT=qT[:, cs], rhs=Sb[b][:, :], start=True, stop=False), r=["qTc_%d" % c, "Sb%d" % b], w=[psk[ob]])
                    op("pe", lambda e: e.matmul(ps[ob][:, 0:256], lhsT=At_all[:, c, :], rhs=va[:, c, :], start=False, stop=True), r=["At_%d" % c, "va_%d" % c], w=[psk[ob]])
                kvb = c % 2
                op("dve", lambda e: e.scalar_tensor_tensor(out=Sf[:, :], in0=Sf[:, :], scalar=dcolT[:, c:c + 1], in1=ps[kvb][:, 0:256],
                                                           op0=ALU.mult, op1=ALU.add), r=["Sf", "dcol_%d" % c, psk[kvb]], w=["Sf"])

            def post_a(c):
                ob = 5 if c % 2 == 0 else 7
                rb = 2 + 2 * (c % 3)
                j = c % 2
                op("act", lambda e: e.activation(out=sr[j][:, :], in_=ps[rb][:, 0:256], func=AF.Exp, scale=-1.0), r=[psk[rb]], w=["sr%d" % j])
                op("dve", lambda e: e.tensor_scalar(out=sr[j][:, :], in0=sr[j][:, :], scalar1=1.0, scalar2=None, op0=ALU.add), r=["sr%d" % j], w=["sr%d" % j])
                op("dve", lambda e: e.reciprocal(out=sr[j][:, :], in_=sr[j][:, :]), r=["sr%d" % j], w=["sr%d" % j])
                op("dve", lambda e: e.tensor_tensor(out=sr[j][:, :], in0=sr[j][:, :], in1=ps[rb][:, 0:256], op=ALU.mult), r=["sr%d" % j, psk[rb]], w=["sr%d" % j])
                op("act", lambda e: e.activation(out=gjunk[:, :], in_=ps[ob][:, 0:256], func=AF.Square, accum_out=gsq[:, c:c + 1]),
                   r=[psk[ob], "gsq"], w=["gsqa_%d" % c, "gjunk"])
                op("act", lambda e: e.activation(out=gsq[:, 16 + c:17 + c], in_=gsq[:, c:c + 1], func=AF.Ln, scale=1.0 / 256, bias=stat[:, 63:64]),
                   r=["gsqa_%d" % c, "epsc"], w=["gsqb_%d" % c])
                op("act", lambda e: e.activation(out=gsq[:, 32 + c:33 + c], in_=gsq[:, 16 + c:17 + c], func=AF.Exp, scale=-0.5), r=["gsqb_%d" % c], w=["gsqc_%d" % c])
                op("dve", lambda e: e.scalar_tensor_tensor(out=osb[j][:, :], in0=ps[ob][:, 0:256], scalar=gsq[:, 32 + c:33 + c],
                                                           in1=ggrow[:, hh * 256:(hh + 1) * 256], op0=ALU.mult, op1=ALU.mult),
                   r=[psk[ob], "gsqc_%d" % c, "ggrow"], w=["osb%d" % j])
                op("dve", lambda e: e.tensor_tensor(out=oab[j][:, :], in0=osb[j][:, :], in1=sr[j][:, :], op=ALU.mult), r=["osb%d" % j, "sr%d" % j], w=["oab%d" % j])

            def post_b(c):
                j = c % 2
                cs = slice(c * 128, (c + 1) * 128)
                p3 = ps[3][:, 0:128].bitcast(BF16)
                for i in range(2):
                    op("pe", lambda e, i=i: e.transpose(out=p3[:, i * 128:(i + 1) * 128], in_=oab[j][:, i * 128:(i + 1) * 128], identity=ident_b),
                       r=["oab%d" % j, "cstb"], w=[psk[3]])
                op("act", lambda e: e.copy(out=oaT[:, 2 * hh:2 * hh + 2, cs], in_=p3.rearrange("p (a b) -> p a b", a=2)),
                   r=[psk[3]], w=["oaT_%d" % c])

            S.muted = (part == "setup")
            if full:
                op("dve", lambda e: e.tensor_copy(out=Sf[:, :], in_=Sst[:, hh, :]), r=["Sst"], w=["Sf"])
            else:
                op("pool", lambda e: e.memset(Sf[:, :], 0.0), w=["Sf"])
            kvmm(0)
            if full:
                rmm(0)
            for c in range(16):
                state(c)
                if c + 1 < 16:
                    kvmm(c + 1)
                    if full:
                        rmm(c + 1)
                if full and c >= 1:
                    post_a(c - 1)
                if full and c >= 2:
                    post_b(c - 2)
            if full:
                post_a(15)
                post_b(14)
                post_b(15)
            if not full:
                op("dve", lambda e: e.tensor_copy(out=glsrc[:, hh * 257:hh * 257 + 256], in_=Sf[:, :]), r=["Sf"], w=["glsrc"])
                op("dve", lambda e: e.reduce_sum(out=gst[:, 4:5], in_=bT[:, :].rearrange("p (c t) -> p c t", t=128)[:, :, 127],
                                                 axis=mybir.AxisListType.X), r=["bT_%d" % c for c in range(16)], w=["gst4"])
                op("act", lambda e: e.activation(out=glsrc[:, hh * 257 + 256:hh * 257 + 257], in_=gst[:, 4:5], func=AF.Exp, scale=1.0 / 16),
                   r=["gst4"], w=["glsrc"])

        op("pool", lambda e: e.memset(glsrc[:, :], 0.0), w=["glsrc"])
        for hh in range(4):
            gla_head(hh, False)
            S.muted = False
        if self.stop_after == "glaH":
            dbg = self.nc.dram_tensor("dbg_sst", [128, 1024], F32, kind="ExternalOutput").ap()
            self.outs.append("dbg_sst")
            op("sp", lambda e: e.dma_start(out=dbg, in_=glsrc[:, 0:1024]), r=["glsrc"], dma="dbg")
            return
        op("sp", lambda e: e.dma_start(out=gl_src, in_=glsrc[:, :]), r=["glsrc"], w=["glsrc_d"], dma="glsrc_d")
        op("pool", lambda e: e.collective_compute("AllGather", ALU.bypass, replica_groups=RG, ins=[gl_src], outs=[gl_g]),
           r=["glsrc_d"], w=["glg_d"], dma="ag_g", inc=1)
        gla_head(0, True, "setup")
        S.muted = False
        op("pool", lambda e: e.memset(Sst[:, :, :], 0.0), w=["Sst"])
        for p in range(4):
            op("sp", lambda e, p=p: e.dma_start(out=glg, in_=gl_g[p * 128:(p + 1) * 128, :]), r=["glg_d"],
               w=["glg", "glsrc"], dma="glg")
            for hh in range(4):
                dcol = glg[:, hh * 257 + 256:hh * 257 + 257]
                vcol = metat[:, p:p + 1]
                op("dve", lambda e, dcol=dcol, vcol=vcol: e.tensor_scalar(out=gst[:, 5:6], in0=dcol, scalar1=-1.0, scalar2=vcol, op0=ALU.add, op1=ALU.mult),
                   r=["glg", "meta"], w=["gst5"])
                op("dve", lambda e: e.tensor_scalar(out=gst[:, 5:6], in0=gst[:, 5:6], scalar1=1.0, scalar2=None, op0=ALU.add), r=["gst5"], w=["gst5"])
                op("dve", lambda e, hh=hh: e.tensor_scalar(out=Sst[:, hh, :], in0=Sst[:, hh, :], scalar1=gst[:, 5:6], scalar2=None, op0=ALU.mult),
                   r=["Sst", "gst5"], w=["Sst"])
                op("dve", lambda e, hh=hh, vcol=vcol: e.scalar_tensor_tensor(out=Sst[:, hh, :], in0=glg[:, hh * 257:hh * 257 + 256], scalar=vcol,
                                                                        in1=Sst[:, hh, :], op0=ALU.mult, op1=ALU.add),
                   r=["glg", "meta", "Sst"], w=["Sst"])
        if self.stop_after == "glaA":
            dbg = self.nc.dram_tensor("dbg_sst", [128, 1024], F32, kind="ExternalOutput").ap()
            self.outs.append("dbg_sst")
            op("sp", lambda e: e.dma_start(out=dbg, in_=Sst[:, :, :].rearrange("p a b -> p (a b)")), r=["Sst"], dma="dbg")
            return
        for hh in range(4):
            gla_head(hh, True, "loop" if hh == 0 else "both")
            S.muted = False
        if self.stop_after == "gla":
            dbg = self.nc.dram_tensor("dbg_oaT", [128, 8 * NTOK], BF16, kind="ExternalOutput").ap()
            self.outs.append("dbg_oaT")
            op("sp", lambda e: e.dma_start(out=dbg, in_=oaT[:, :, :].rearrange("p k n -> p (k n)")),
               r=["oaT_%d" % a for a in range(16)], dma="dbg")
            return
        S.barrier()

        S.muted = "fox" in self.skip
        self.off = mark2
        Kaug = [self.sb("Kaug%d" % i, [128, 4 * NTOK], BF16) for i in range(2)]
        Vaug = [self.sb("Vaug%d" % i, [128, 64, 128], BF16) for i in range(2)]
        Qaug = [self.sb("Qaug%d" % i, [128, NTOK], BF16) for i in range(2)]
        pT = [self.sb("pT%d" % i, [128, 512], BF16) for i in range(4)]
        KB0 = self.sb("KB0", [128, 64, 16], F32)
        flg = self.sb("flg", [128, 4, 256], F32)
        totrep = self.sb("totrep", [128, 4, 16], F32)
        dp = self.sb("dp", [128, 4, 16], F32)
        fsel = self.sb("fsel", [128, 16, 64], F32)
        kbq = [self.sb("kbq%d" % i, [128, 64], F32) for i in range(2)]
        rcp = gp_row[0][:, 0:512]
        wq = [self.sb("wq%d" % i, [128, KC, 64], BF16) for i in range(2)]
        E127 = cst[:, 384:512]

        for sl in range(2):
            op("pool", lambda e, sl=sl: e.memset(Kaug[sl][0:64, :], 0.0), w=["KaugI%d" % sl])
            op("pool", lambda e, sl=sl: e.memset(Kaug[sl][0:1, :], 1.0), w=["KaugI%d" % sl])
            op("pool", lambda e, sl=sl: e.memset(Kaug[sl][32:33, :], 1.0), w=["KaugI%d" % sl])
            op("pool", lambda e, sl=sl: e.memset(Vaug[sl][:, :, 64:128], 1.0), w=["VaugI%d" % sl])
            op("pool", lambda e, sl=sl: e.memset(Qaug[sl][0:64, :], 0.0), w=["QaugI%d" % sl])
        op("pool", lambda e: e.memset(fsel[:, :, :], 0.0), w=["fsel"])
        op("sp", lambda e: e.dma_start(out=flg[:, :, :], in_=fl_g.rearrange("(s p) c -> p s c", p=128)), r=["flg_d"], w=["flg"], dma="flg")
        for p in range(4):
            op("pe", lambda e, p=p: e.matmul(ps[7][:, p * 16:(p + 1) * 16], lhsT=E127, rhs=flg[:, p, 240:256], start=True, stop=True),
               r=["flg", "cst"], w=[psk[7]])
        op("dve", lambda e: e.tensor_copy(out=totrep[:, :, :].rearrange("p a b -> p (a b)"), in_=ps[7][:, 0:64]), r=[psk[7]], w=["totrep"])
        for p in (3, 2, 1, 0):
            op("dve", lambda e, p=p: e.tensor_scalar(out=dp[:, p, :], in0=totrep[:, p, :], scalar1=metat[:, p:p + 1], scalar2=None, op0=ALU.mult),
               r=["totrep", "meta"], w=["dp%d" % p])
            if p < 3:
                op("dve", lambda e, p=p: e.tensor_tensor(out=dp[:, p, :], in0=dp[:, p, :], in1=dp[:, p + 1, :], op=ALU.add),
                   r=["dp%d" % p, "dp%d" % (p + 1)], w=["dp%d" % p])
        for p in range(3):
            for blk in range(16):
                op("dve", lambda e, p=p, blk=blk: e.scalar_tensor_tensor(
                    out=KB0[:, p * 16 + blk, :], in0=dp[:, p, :], scalar=metat[:, 4 + p:5 + p], in1=flg[:, p, blk * 16:(blk + 1) * 16],
                    op0=ALU.add, op1=ALU.subtract), r=["dp%d" % p_ for p_ in range(4)] + ["flg", "meta"], w=["KB0"])
        op("dve", lambda e: e.tensor_scalar(out=KB0[:, 48:64, :].rearrange("p a b -> p (a b)"), in0=floc[:, :], scalar1=-1.0, scalar2=None, op0=ALU.mult),
           r=["floc"], w=["KB0"])

        def fox_prep(h):
            sl, hp, hs = h % 2, h // 2, h % 2
            for p in range(3):
                r0 = p * 256 + (hp % 2) * 128 + hs * 64
                kq = hp // 2
                op("sp", lambda e, sl=sl, p=p, r0=r0, kq=kq: e.dma_start(out=Kaug[sl][64:128, p * NTOK:(p + 1) * NTOK], in_=kT_g[kq][r0:r0 + 64, :]),
                   r=["kTg%d" % kq], w=["Ka%d_%d" % (sl, p)], dma="ka%d_%d" % (sl, p))
                for vq in range(4):
                    op("sp", lambda e, sl=sl, p=p, vq=vq, h=h: e.dma_start(
                        out=Vaug[sl][:, p * 16 + 4 * vq:p * 16 + 4 * vq + 4, 0:64],
                        in_=v_g[vq][p * 512:(p + 1) * 512, h * 64:(h + 1) * 64].rearrange("(b p) d -> p b d", p=128)),
                       r=["vg%d" % vq, "VaugI%d" % sl], w=["Va%d_%d" % (sl, p)], dma="va%d_%d" % (sl, p))
            r0 = hp * 128 + hs * 64
            op("sp", lambda e, sl=sl, r0=r0: e.dma_start(out=Kaug[sl][64:128, 3 * NTOK:4 * NTOK], in_=kT_src[r0:r0 + 64, :]),
               r=["kTsrc_%d" % hp], w=["Ka%d_4" % sl], dma="ka%d_4" % sl)
            op("sp", lambda e, sl=sl, h=h: e.dma_start(out=Vaug[sl][:, 48:64, 0:64],
                                                      in_=v_src[:, h * 64:(h + 1) * 64].rearrange("(b p) d -> p b d", p=128)),
               r=["vsrc_%d" % i for i in range(16)] + ["VaugI%d" % sl], w=["Va%d_4" % sl], dma="va%d_4" % sl)
            op("pool", lambda e, sl=sl, h=h: e.dma_start(out=wq[sl][:, :, :], in_=win_v[:, :, O_QB + h * 64:O_QB + (h + 1) * 64]),
               w=["wq%d" % sl], dma="wq%d" % sl)
            for tg in range(4):
                for kc in range(KC):
                    op("pe", lambda e, kc=kc, sl=sl, tg=tg: e.matmul(
                        ps[5][64:128, :], lhsT=wq[sl][:, kc, :], rhs=h2T[:, kc, tg * 512:(tg + 1) * 512],
                        start=(kc == 0), stop=(kc == KC - 1)), r=["wq%d" % sl] + h2k(tg * 4, tg * 4 + 4), w=[psk[5]])
                op("dve", lambda e, sl=sl, tg=tg: e.tensor_scalar(out=Qaug[sl][64:128, tg * 512:(tg + 1) * 512], in0=ps[5][64:128, :],
                                                                 scalar1=0.125, scalar2=None, op0=ALU.mult),
                   r=[psk[5]], w=["Qq%d_%d" % (sl, tg)])
            rall_h = rall[:, :].rearrange("p (b h) -> p b h", h=16)[:, :, h]
            op("dve", lambda e, rall_h=rall_h: e.tensor_copy(out=fsel[:, :, 0], in_=rall_h), r=["rall"], w=["fsel"])
            op("dve", lambda e, rall_h=rall_h: e.tensor_copy(out=fsel[:, :, 32], in_=rall_h), r=["rall"], w=["fsel"])
            for qt in range(4):
                for i in range(4):
                    blk = 4 * qt + i
                    op("pe", lambda e, blk=blk, i=i: e.matmul(ps[6][0:64, i * 128:(i + 1) * 128], lhsT=fsel[:, blk, :], rhs=ident_f,
                                                              start=True, stop=True), r=["fsel", "cst"], w=[psk[6]])
                op("act", lambda e, sl=sl, qt=qt: e.copy(out=Qaug[sl][0:64, qt * 512:(qt + 1) * 512], in_=ps[6][0:64, :]),
                   r=[psk[6], "QaugI%d" % sl], w=["Qr%d_%d" % (sl, qt)])
                op("dve", lambda e, sl=sl, qt=qt: e.tensor_tensor(out=Qaug[sl][32:64, qt * 512:(qt + 1) * 512], in0=ps[6][32:64, :],
                                                                 in1=Qaug[sl][32:64, qt * 512:(qt + 1) * 512], op=ALU.subtract),
                   r=[psk[6], "Qr%d_%d" % (sl, qt)], w=["Qr%d_%d" % (sl, qt)])
            KB0_h = KB0[:, :, h]
            for qt in range(4):
                kb_ = gp_row[0][:, 512 + sl * 256 + qt * 64:512 + sl * 256 + (qt + 1) * 64]
                op("dve", lambda e, qt=qt, h=h, KB0_h=KB0_h, kb_=kb_: e.tensor_scalar(
                    out=kb_, in0=KB0_h, scalar1=cq[:, qt * 16 + h:qt * 16 + h + 1], scalar2=None, op0=ALU.add),
                   r=["KB0", "cq0", "cq1"], w=["kbq%d_%d" % (sl, qt)])

        def fox_attn(h, hook):
            sl, hp, hs = h % 2, h // 2, h % 2
            KB0_h = KB0[:, :, h]
            kbq4 = [gp_row[0][:, 512 + sl * 256 + q_ * 64:512 + sl * 256 + (q_ + 1) * 64] for q_ in range(4)]
            items = []
            for qt in range(4):
                blocks = [(p * 16 + b, False, 0, p) for p in range(3) for b in range(16)]
                blocks += [(48 + b, b >= 4 * qt, b - 4 * qt, 4) for b in range(4 * qt + 4)]
                for n, blk in enumerate(blocks):
                    items.append((qt, n, len(blocks)) + blk)
            DEPTH = 2
            nit = len(items)

            def emit_S(idx):
                qt, n, nb, bi, diag, i, p = items[idx]
                c0 = i * 128 if diag else 0
                sbk = 2 + idx % 3
                op("pe", lambda e, sl=sl, bi=bi, c0=c0, sbk=sbk, qt=qt: e.matmul(
                    ps[sbk][:, c0:512], lhsT=Kaug[sl][:, bi * 128:(bi + 1) * 128], rhs=Qaug[sl][:, qt * 512 + c0:(qt + 1) * 512],
                    start=True, stop=True),
                   r=["Ka%d_%d" % (sl, p if p < 4 else 4), "KaugI%d" % sl, "Qq%d_%d" % (sl, qt), "Qr%d_%d" % (sl, qt)], w=[psk[sbk]])

            for idx in range(min(DEPTH, nit)):
                emit_S(idx)
            for idx, (qt, n, nb, bi, diag, i, p) in enumerate(items):
                if idx == nit // 4 and hook is not None:
                    hook()
                c0 = i * 128 if diag else 0
                sbk = 2 + idx % 3
                pb = idx % 4
                acc = qt % 2
                op("act", lambda e, sbk=sbk, pb=pb, c0=c0, qt=qt, bi=bi: e.activation(
                    out=pT[pb][:, c0:512], in_=ps[sbk][:, c0:512], func=AF.Exp, bias=kbq4[qt][:, bi:bi + 1], scale=1.0),
                   r=[psk[sbk], "kbq%d_%d" % (sl, qt)], w=["pT%d" % pb])
                if diag:
                    op("dve", lambda e, pb=pb, c0=c0: e.tensor_tensor(out=pT[pb][:, c0:c0 + 128], in0=pT[pb][:, c0:c0 + 128],
                                                                     in1=U_b, op=ALU.mult), r=["pT%d" % pb, "cstb"], w=["pT%d" % pb])
                if idx + DEPTH < nit:
                    emit_S(idx + DEPTH)
                op("pe", lambda e, sl=sl, bi=bi, c0=c0, pb=pb, acc=acc, n=n, nb=nb: e.matmul(
                    ps[acc][:, c0:512], lhsT=Vaug[sl][:, bi, :], rhs=pT[pb][:, c0:512], start=(n == 0), stop=(n == nb - 1)),
                   r=["Va%d_%d" % (sl, p if p < 4 else 4), "VaugI%d" % sl, "pT%d" % pb], w=[psk[acc]])
                if n == nb - 1:
                    op("dve", lambda e, acc=acc: e.reciprocal(out=rcp[0:64, :], in_=ps[acc][64:128, :]), r=[psk[acc]], w=["rcp"])
                    op("dve", lambda e, acc=acc, hs=hs, hp=hp, qt=qt: e.tensor_tensor(
                        out=obT[hs * 64:(hs + 1) * 64, hp, qt * 512:(qt + 1) * 512], in0=ps[acc][0:64, :], in1=rcp[0:64, :], op=ALU.mult),
                       r=[psk[acc], "rcp"], w=["obT_%d_%d" % (hp, qt)])
        fox_prep(0)
        for h in range(16):
            fox_attn(h, (lambda h=h: fox_prep(h + 1)) if h + 1 < 16 else None)
        if self.stop_after == "fox":
            dbg = self.nc.dram_tensor("dbg_obT", [128, 8 * NTOK], BF16, kind="ExternalOutput").ap()
            self.outs.append("dbg_obT")
            op("sp", lambda e: e.dma_start(out=dbg, in_=obT[:, :, :].rearrange("p k n -> p (k n)")),
               r=["obT_%d_%d" % (a, b) for a in range(8) for b in range(4)], dma="dbg")
            return
        S.barrier()

        S.muted = False
        self.off = mark3
        w5 = {}
        for nm, src in (("wga", win_v[:, :, O_GA:O_GA + D]), ("wgb", win_v[:, :, O_GB:O_GB + D]),
                        ("wpa", w_pa.rearrange("(kc p) n -> p kc n", p=128)), ("wpb", w_pb.rearrange("(kc p) n -> p kc n", p=128)),
                        ("wo", w_out.rearrange("(kc p) n -> p kc n", p=128))):
            w5[nm] = self.sb(nm, [128, KC, D], BF16)
            op("pool", lambda e, nm=nm, src=src: e.dma_start(out=w5[nm][:, :, :], in_=src), w=[nm], dma=nm)
        t5 = self.sb("t5", [128, D], F32)
        mT = self.sb("mT", [128, KC, 128], BF16)
        xs5 = [self.sb("xs5", [128, D], F32)]
        xo5 = xs5[0]
        m1 = gp_row[0]
        self.off = mark + 32768
        sga = self.sb("sga", [128, D], BF16)
        mb = self.sb("mb", [128, D], BF16)
        junk5 = sga
        for ti in range(NT):
            cs = slice(ti * 128, (ti + 1) * 128)
            for gi, (gw, pw, srcT, skey) in enumerate((("wga", "wpa", oaT, "oaT_%d" % ti), ("wgb", "wpb", obT, None))):
                for half in range(2):
                    hsl = slice(half * 512, (half + 1) * 512)
                    for kc in range(KC):
                        op("pe", lambda e, kc=kc, half=half, hsl=hsl, gw=gw, cs=cs: e.matmul(
                            ps[half][:, :], lhsT=h2T[:, kc, cs], rhs=w5[gw][:, kc, hsl], start=(kc == 0), stop=(kc == KC - 1)),
                           r=[gw, "h2T_%d" % ti], w=[psk[half]])
                    op("act", lambda e, half=half, hsl=hsl: e.activation(out=sga[:, hsl], in_=ps[half][:, :], func=AF.Sigmoid),
                       r=[psk[half]], w=["sga%d" % half])
                    rk = [pw] + ([skey] if skey else ["obT_%d_%d" % (a, ti // 4) for a in range(8)])
                    for kc in range(KC):
                        op("pe", lambda e, kc=kc, half=half, hsl=hsl, pw=pw, srcT=srcT, cs=cs: e.matmul(
                            ps[2 + half][:, :], lhsT=srcT[:, kc, cs], rhs=w5[pw][:, kc, hsl], start=(kc == 0), stop=(kc == KC - 1)),
                           r=rk, w=[psk[2 + half]])
                    if gi == 0:
                        op("dve", lambda e, half=half, hsl=hsl: e.tensor_tensor(out=m1[:, hsl], in0=ps[2 + half][:, :], in1=sga[:, hsl], op=ALU.mult),
                           r=[psk[2 + half], "sga%d" % half], w=["m1_%d" % half])
                    else:
                        op("dve", lambda e, half=half, hsl=hsl: e.tensor_tensor(out=t5[:, hsl], in0=ps[2 + half][:, :], in1=sga[:, hsl], op=ALU.mult),
                           r=[psk[2 + half], "sga%d" % half], w=["t5x"])
                        op("pool", lambda e, half=half, hsl=hsl: e.tensor_tensor(out=mb[:, hsl], in0=m1[:, hsl], in1=t5[:, hsl], op=ALU.add),
                           r=["m1_%d" % half, "t5x"], w=["mb%d" % half])
            p6 = ps[6][:, 0:512].bitcast(BF16)
            for kc in range(KC):
                op("pe", lambda e, kc=kc: e.transpose(out=p6[:, kc * 128:(kc + 1) * 128], in_=mb[:, kc * 128:(kc + 1) * 128], identity=ident_b),
                   r=["mb0", "mb1", "cstb"], w=[psk[6]])
            op("act", lambda e: e.copy(out=mT[:, :, :].rearrange("p a b -> p (a b)"), in_=p6), r=[psk[6]], w=["mT"])
            for half in range(2):
                hsl = slice(half * 512, (half + 1) * 512)
                for kc in range(KC):
                    op("pe", lambda e, kc=kc, half=half, hsl=hsl: e.matmul(ps[4 + half][:, :], lhsT=mT[:, kc, :], rhs=w5["wo"][:, kc, hsl],
                                                                         start=(kc == 0), stop=(kc == KC - 1)), r=["mT", "wo"], w=[psk[4 + half]])
            post_res(4, 5, 1, ti, x1d, x2d, junk5, t5, "t5x", xs5, xo5, "xs0", None)
        if self.stop_after == "mix":
            return
        S.barrier()

        B2 = alloc_ffn()
        ffn(2, 1, x2d, out, None, B2)


_CACHE = {}


def _consts():
    ident = np.eye(128, dtype=np.float32)
    U = np.triu(np.ones((128, 128), np.float32))
    ones = np.ones((128, 128), np.float32)
    e127 = np.zeros((128, 128), np.float32)
    e127[127, :] = 1.0
    return np.concatenate([ident, U, ones, e127], axis=1)


def make_in_maps(inputs):
    f = lambda a: np.ascontiguousarray(np.asarray(a, dtype=np.float32))
    x = f(inputs["x"])
    maps = []
    shared = {
        "w_ada": f(inputs["w_ada"][0]), "b_ada": f(inputs["b_ada"]).reshape(1, -1),
        "g_pre": f(inputs["g_pre"]).reshape(1, -1), "g_post": f(inputs["g_post"]).reshape(1, -1),
        "w_gu1": f(inputs["w_gu1"][0]), "w_dn1": f(inputs["w_dn1"][0]),
        "w_gu2": f(inputs["w_gu2"][0]), "w_dn2": f(inputs["w_dn2"][0]),
        "w_in": f(inputs["w_in"][0]), "w_a2": f(inputs["w_a2"][0]), "b_a": f(inputs["b_a"]).reshape(1, -1),
        "b_f": f(inputs["b_f"]).reshape(1, -1), "g_gla": f(inputs["g_gla"]).reshape(1, -1),
        "w_pa": f(inputs["w_pa"][0]), "w_pb": f(inputs["w_pb"][0]), "w_out": f(inputs["w_out"][0]),
        "consts": _consts(),
    }
    c = f(inputs["c"])
    for core in range(NCORES):
        b, j = core // GRP, core % GRP
        meta = np.zeros((128, 8), np.float32)
        for p in range(4):
            meta[:, p] = 1.0 if p < j else 0.0
            meta[:, 4 + p] = 0.0 if p < j else KILL
        m = dict(shared)
        m["x"] = np.ascontiguousarray(x[b, j * NTOK:(j + 1) * NTOK, :])
        m["c"] = np.ascontiguousarray(c[b:b + 1, :])
        m["meta"] = meta
        maps.append(m)
    return maps


def kernel(**inputs):
    if "nc" not in _CACHE:
        _CACHE["nc"] = Builder().build()
    nc = _CACHE["nc"]
    maps = make_in_maps(inputs)
    res = run_bass_kernel_spmd(nc, maps, core_ids=list(range(NCORES)))
    out = np.zeros((2, 8192, D), np.float32)
    for core in range(NCORES):
        b, j = core // GRP, core % GRP
        out[b, j * NTOK:(j + 1) * NTOK, :] = res.results[core]["out"]
    return out
```

```python
import numpy as np
import concourse.bass as bass
import concourse.mybir as mybir
from concourse.bass_utils import run_bass_kernel_spmd

F32 = mybir.dt.float32
BF16 = mybir.dt.bfloat16
AF = mybir.ActivationFunctionType
ALU = mybir.AluOpType

D = 1024
KC = 8
NTOK = 2048
NT = 16
DFF = 2816
HC = 22
EPS = 1e-6
NCORES = 8
GRP = 4
W_IN_COLS = 8224
O_QA, O_KA, O_VA, O_AL, O_RA, O_QB, O_KB, O_VB, O_FB, O_GA, O_GB = (
    0, 512, 1024, 2048, 2064, 3088, 4112, 5136, 6160, 6176, 7200)
KILL = -30000.0


class Sched:
    ENGS = ("pe", "act", "dve", "pool", "sp")

    def __init__(self, nc):
        self.nc = nc
        self.ops = []
        self.last_writer = {}
        self.readers = {}
        self.eng_ops = {e: [] for e in self.ENGS}
        self.dma_groups = {}
        self.barriers = []
        self.bar_skip = []

    muted = False
    attach_waits = True

    def op(self, eng, fn, r=(), w=(), dma=None, inc=16):
        if self.muted:
            return None
        idx = len(self.ops)
        deps = set()
        for k in r:
            lw = self.last_writer.get(k)
            if lw is not None:
                deps.add(lw)
        for k in w:
            lw = self.last_writer.get(k)
            if lw is not None:
                deps.add(lw)
            for ridx in self.readers.get(k, {}).values():
                deps.add(ridx)
        if dma is not None:
            g = self.dma_groups.setdefault(dma, [])
            if g:
                deps.add(g[-1])
            g.append(idx)
        for k in w:
            self.last_writer[k] = idx
            self.readers[k] = {}
        rk = ("dma", idx) if dma is not None else eng
        for k in r:
            self.readers.setdefault(k, {})[rk] = idx
        deps.discard(idx)
        o = dict(eng=eng, fn=fn, deps=deps, dma=dma, inc=inc, signal=False,
                 pos=len(self.eng_ops[eng]), bar=len(self.barriers))
        self.ops.append(o)
        self.eng_ops[eng].append(idx)
        return idx

    def barrier(self, skip_prefix=None):
        self.barriers.append(len(self.ops))
        self.bar_skip.append(skip_prefix)

    def _needs_sync(self, o, d):
        if d["dma"] is not None:
            return True
        if d["eng"] != o["eng"]:
            return True
        if o["eng"] == "pe":
            return False
        if o["dma"] is not None:
            return True
        return (o["pos"] - d["pos"]) <= 3

    def emit(self):
        nc = self.nc
        ops = self.ops
        bar_last = []
        for bpos in self.barriers:
            last = {}
            for e in self.ENGS:
                cand = [i for i in self.eng_ops[e] if i < bpos and ops[i]["dma"] is None]
                if cand:
                    last[e] = cand[-1]
            dl = {}
            skip = self.bar_skip[len(bar_last)]
            for gname, lst in self.dma_groups.items():
                if skip is not None and gname.startswith(skip):
                    continue
                cand = [i for i in lst if i < bpos]
                if cand:
                    dl[gname] = cand[-1]
            bar_last.append((last, dl))
        for o in ops:
            for d in o["deps"]:
                if ops[d]["dma"] is None and self._needs_sync(o, ops[d]):
                    ops[d]["signal"] = True
        for last, _ in bar_last:
            for e, i in last.items():
                ops[i]["signal"] = True
        cnt = {e: 0 for e in self.ENGS}
        for o in ops:
            if o["dma"] is None and o["signal"]:
                cnt[o["eng"]] += 1
                o["val"] = cnt[o["eng"]]
        for gname, lst in self.dma_groups.items():
            v = 0
            for i in lst:
                v += ops[i]["inc"]
                ops[i]["val"] = v
        import contextlib
        stack = contextlib.ExitStack()
        sems = {}
        for e in self.ENGS:
            sems[e] = stack.enter_context(nc.semaphore("s_" + e))
        gsem = {}
        for gi, gname in enumerate(self.dma_groups):
            gsem[gname] = stack.enter_context(nc.semaphore("g%d" % gi))
        engobj = {"pe": nc.tensor, "act": nc.scalar, "dve": nc.vector,
                  "pool": nc.gpsimd, "sp": nc.sync}
        final_waits = []
        for gname, lst in self.dma_groups.items():
            final_waits.append((gsem[gname], ops[lst[-1]]["val"]))

        def run_engine(ename, eng):
            waited = {}

            def wait(sem, val, key):
                if waited.get(key, 0) >= val:
                    return
                waited[key] = val
                eng.wait_ge(sem, val)

            nbar = 0
            for idx in self.eng_ops[ename]:
                o = ops[idx]
                while nbar < o["bar"]:
                    last, dl = bar_last[nbar]
                    for e, i in last.items():
                        if e != ename or ename != "pe":
                            wait(sems[e], ops[i]["val"], e)
                    for gname, i in dl.items():
                        wait(gsem[gname], ops[i]["val"], "g" + gname)
                    nbar += 1
                need = {}
                for d in sorted(o["deps"]):
                    dd = ops[d]
                    if not self._needs_sync(o, dd):
                        continue
                    if dd["dma"] is not None:
                        key, sem = "g" + dd["dma"], gsem[dd["dma"]]
                    else:
                        key, sem = dd["eng"], sems[dd["eng"]]
                    if waited.get(key, 0) >= dd["val"]:
                        continue
                    if key not in need or need[key][1] < dd["val"]:
                        need[key] = (sem, dd["val"])
                attach = None
                if self.attach_waits and need and o["dma"] is None:
                    k_last = sorted(need)[-1]
                    attach = need.pop(k_last)
                    waited[k_last] = attach[1]
                for key in sorted(need):
                    wait(need[key][0], need[key][1], key)
                ins = o["fn"](eng)
                if attach is not None:
                    ins._wait_ge(attach[0], attach[1])
                if o["dma"] is not None:
                    ins.then_inc(gsem[o["dma"]], o["inc"])
                elif o["signal"]:
                    ins.then_inc(sems[ename], 1)
            if ename == "sp":
                for sem, val in final_waits:
                    eng.wait_ge(sem, val)

        with stack:
            with nc.Block() as block:
                @block.tensor
                def _(e):
                    run_engine("pe", e)

                @block.scalar
                def _(e):
                    run_engine("act", e)

                @block.vector
                def _(e):
                    run_engine("dve", e)

                @block.gpsimd
                def _(e):
                    run_engine("pool", e)

                @block.sync
                def _(e):
                    run_engine("sp", e)


class Builder:
    def __init__(self, debug=(), stop_after=None, skip=()):
        self.skip = set(skip)
        self.cut = 0
        self.v = ""
        self.debug = set(debug)
        self.stop_after = stop_after
        self.nc = bass.Bass("TRN2", target_bir_lowering=False)
        self.S = Sched(self.nc)
        self._stack = None
        self.outs = []
        self.off = 16512
        self.hw = 0

    def din(self, name, shape, dt=F32):
        return self.nc.dram_tensor(name, list(shape), dt, kind="ExternalInput").ap()

    def dscratch(self, name, shape, dt):
        kind = "ExternalOutput" if name in self.debug else "Internal"
        if kind == "ExternalOutput":
            self.outs.append(name)
        return self.nc.dram_tensor(name, list(shape), dt, kind=kind).ap()

    def sb(self, name, shape, dt):
        isz = 4 if dt == F32 else 2
        n = 1
        for d_ in shape[1:]:
            n *= d_
        nbytes = (n * isz + 31) // 32 * 32
        off = self.off
        assert off + nbytes <= 229344, ("SBUF overflow", name, off, nbytes)
        self.off += nbytes
        self.hw = max(self.hw, self.off)
        return self.nc.alloc_sbuf_tensor_at(name, list(shape), dt, offset=off)

    def psum(self, name, shape, dt=F32):
        return self._stack.enter_context(self.nc.psum_tensor(name, list(shape), dt))

    def build(self):
        import contextlib
        with contextlib.ExitStack() as st:
            self._stack = st
            self._program()
            self.S.emit()
        return self.nc

    def _program(self):
        nc, S = self.nc, self.S
        op = S.op
        x = self.din("x", [NTOK, D])
        c_in = self.din("c", [1, D])
        w_ada = self.din("w_ada", [D, 9 * D])
        b_ada = self.din("b_ada", [1, 9 * D])
        g_pre = self.din("g_pre", [1, 3 * D])
        g_post = self.din("g_post", [1, 3 * D])
        w_gu = [self.din("w_gu1", [D, 2 * DFF]), self.din("w_gu2", [D, 2 * DFF])]
        w_dn = [self.din("w_dn1", [DFF, D]), self.din("w_dn2", [DFF, D])]
        w_in = self.din("w_in", [D, W_IN_COLS])
        w_a2 = self.din("w_a2", [16, 512])
        b_a = self.din("b_a", [1, 512])
        b_f = self.din("b_f", [1, 16])
        g_gla = self.din("g_gla", [1, D])
        w_pa = self.din("w_pa", [D, D])
        w_pb = self.din("w_pb", [D, D])
        w_out = self.din("w_out", [D, D])
        consts = self.din("consts", [128, 512])
        meta = self.din("meta", [128, 8])
        out = self.nc.dram_tensor("out", [NTOK, D], F32, kind="ExternalOutput").ap()
        x1d = self.dscratch("x1d", [NTOK, D], F32)
        x2d = self.dscratch("x2d", [NTOK, D], F32)

        cst = self.sb("cst", [128, 512], F32)
        ident_f = cst[:, 0:128]
        U_f = cst[:, 128:256]
        ones_f = cst[:, 256:384]
        cstb = self.sb("cstb", [128, 384], BF16)
        ident_b = cstb[:, 0:128]
        U_b = cstb[:, 128:256]
        ones_b = cstb[:, 256:384]
        metat = self.sb("metat", [128, 8], F32)
        vstage = self.sb("vstage", [128, 128], F32)
        modc = self.sb("modc", [128, 72], F32)
        gcol = self.sb("gcol", [128, 48], F32)
        scale_c = self.sb("scale_c", [128, 24], F32)
        gate_c = self.sb("gate_c", [128, 24], F32)
        gp_row = [self.sb("gp_row%d" % i, [128, D], F32) for i in range(3)]
        stat = self.sb("stat", [128, 96], F32)
        sc_col = self.sb("sc_col", [128, 8], BF16)

        ps = [self.psum("ps%d" % i, [128, 512], F32) for i in range(8)]
        psk = ["ps%d" % i for i in range(8)]

        h2T = self.sb("h2T", [128, KC, NTOK], BF16)
        mark = self.off

        gen = [0]

        def alloc_ffn():
            gen[0] += 1
            g = "_%d" % gen[0]
            self.off = mark
            B = {}
            B["hT"] = self.sb("hT" + g, [128, KC, 1024], BF16)
            B["actT"] = self.sb("actT" + g, [128, HC, 1024], BF16)
            B["wdn"] = self.sb("wdn" + g, [128, HC, D], BF16)
            B["wgu"] = [self.sb("wgu%d" % i + g, [128, KC, 256], BF16) for i in range(3)]
            B["xs"] = [self.sb("xs%d" % i + g, [128, D], F32) for i in range(3)]
            B["xn"] = [self.sb("xn%d" % i + g, [128, D], BF16) for i in range(2)]
            B["junk"] = self.sb("junk" + g, [128, D], BF16)
            B["tmp"] = [self.sb("tmp%d" % i + g, [128, D], F32) for i in range(2)]
            B["xo"] = [self.sb("xo%d" % i + g, [128, D], F32) for i in range(2)]
            B["sg"] = [self.sb("sg%d" % i + g, [128, 512], F32) for i in range(2)]
            return B

        B = alloc_ffn()
        actT, tmp = B["actT"], B["tmp"]

        op("sp", lambda e: e.dma_start(out=cst[:, :], in_=consts), w=["cst"], dma="cst")
        op("sp", lambda e: e.dma_start(out=metat[:, :], in_=meta), w=["meta"], dma="meta")
        op("dve", lambda e: e.tensor_copy(out=cstb[:, :], in_=cst[:, 0:384]), r=["cst"], w=["cstb"])
        op("pool", lambda e: e.memset(stat[:, 63:64], EPS), w=["epsc"])
        op("pool", lambda e: e.memset(stat[:, 62:63], 1.0), w=["onec"])
        vcnt = [0]

        def vec_to_cols(vec_ap, n, dst, wkey, bank, func=None):
            vcnt[0] += 1
            op("sp", lambda e: e.dma_start(out=vstage[0:n, :], in_=vec_ap.rearrange("o (j p) -> (o j) p", p=128)),
               w=["vstage"], dma="vstage")
            op("pe", lambda e: e.transpose(out=ps[bank][:, 0:n], in_=vstage[0:n, :], identity=ident_f[0:n, 0:n]),
               r=["vstage", "cst"], w=[psk[bank]])
            if func is None:
                op("dve", lambda e: e.tensor_copy(out=dst, in_=ps[bank][:, 0:n]), r=[psk[bank]], w=[wkey])
            else:
                op("act", lambda e: e.activation(out=dst, in_=ps[bank][:, 0:n], func=func), r=[psk[bank]], w=[wkey])

        vec_to_cols(c_in, 8, sc_col[:, :], "sc_col", 0, func=AF.Silu)
        vec_to_cols(g_pre, 24, gcol[:, 0:24], "gcol_a", 1)
        vec_to_cols(g_post, 24, gcol[:, 24:48], "gcol_b", 0)

        wada_v = w_ada.rearrange("(kc p) n -> p kc n", p=128)
        wa = [actT[:, 0:8, :], actT[:, 8:16, :]]
        wak = ["wa0", "wa1"]
        for i in range(9):
            sl = i % 2
            op("pool", lambda e, i=i, sl=sl: e.dma_start(out=wa[sl], in_=wada_v[:, :, i * D:(i + 1) * D]),
               w=[wak[sl]], dma=wak[sl])
            for m in range(8):
                for kc in range(KC):
                    op("pe", lambda e, i=i, m=m, kc=kc, sl=sl: e.matmul(
                        ps[2][:, i * 8 + m:i * 8 + m + 1], lhsT=wa[sl][:, kc, m * 128:(m + 1) * 128],
                        rhs=sc_col[:, kc:kc + 1], start=(kc == 0), stop=(kc == KC - 1)),
                       r=[wak[sl], "sc_col"], w=[psk[2]])
        vec_to_cols(b_ada, 72, modc[:, :], "modc_b", 3)
        op("dve", lambda e: e.tensor_tensor(out=modc[:, :], in0=modc[:, :], in1=ps[2][:, 0:72], op=ALU.add),
           r=["modc_b", psk[2]], w=["modc"])
        for i in range(3):
            sc = modc[:, (3 * i + 1) * 8:(3 * i + 2) * 8]
            gt = modc[:, (3 * i + 2) * 8:(3 * i + 3) * 8]
            op("dve", lambda e, i=i, sc=sc: e.scalar_tensor_tensor(
                out=scale_c[:, i * 8:(i + 1) * 8], in0=sc, scalar=1.0, in1=gcol[:, i * 8:(i + 1) * 8],
                op0=ALU.add, op1=ALU.mult), r=["modc", "gcol_a", "gcol_b"], w=["scale_c"])
            fac = 1.0 if i == 1 else 0.5
            op("dve", lambda e, i=i, gt=gt, fac=fac: e.scalar_tensor_tensor(
                out=gate_c[:, i * 8:(i + 1) * 8], in0=gt, scalar=fac, in1=gcol[:, 24 + i * 8:24 + (i + 1) * 8],
                op0=ALU.mult, op1=ALU.mult), r=["modc", "gcol_a", "gcol_b"], w=["gate_c"])
        for i in range(3):
            for m in range(8):
                op("dve", lambda e, i=i, m=m: e.tensor_scalar(
                    out=tmp[0][:, m * 128:(m + 1) * 128], in0=ident_f, scalar1=gate_c[:, i * 8 + m:i * 8 + m + 1],
                    scalar2=None, op0=ALU.mult), r=["gate_c", "cst"], w=["tmp0"])
            for hfi in range(2):
                for m in range(4):
                    mm = hfi * 4 + m
                    op("pe", lambda e, hfi=hfi, m=m, mm=mm: e.matmul(
                        ps[4 + hfi][:, m * 128:(m + 1) * 128], lhsT=ones_f, rhs=tmp[0][:, mm * 128:(mm + 1) * 128],
                        start=True, stop=True), r=["tmp0", "cst"], w=[psk[4 + hfi]])
                op("act", lambda e, i=i, hfi=hfi: e.copy(out=gp_row[i][:, hfi * 512:(hfi + 1) * 512], in_=ps[4 + hfi][:, :]),
                   r=[psk[4 + hfi]], w=["gp_row%d" % i])
        if self.stop_after == "mods":
            dbg = self.nc.dram_tensor("dbg_mods", [128, 72 + 24 + 24 + 1024], F32, kind="ExternalOutput").ap()
            self.outs.append("dbg_mods")
            op("sp", lambda e: e.dma_start(out=dbg[:, 0:72], in_=modc[:, :]), r=["modc"], dma="dbg")
            op("sp", lambda e: e.dma_start(out=dbg[:, 72:96], in_=scale_c[:, :]), r=["scale_c"], dma="dbg")
            op("sp", lambda e: e.dma_start(out=dbg[:, 96:120], in_=gate_c[:, :]), r=["gate_c"], dma="dbg")
            op("sp", lambda e: e.dma_start(out=dbg[:, 120:120 + 1024], in_=gp_row[1][:, :]), r=["gp_row1"], dma="dbg")
            return
        S.barrier()

        cnt = {"xs": 0, "xn": 0, "st": 0, "pst": 0}

        def rstd_from(ssq_aps, dst, rkeys, wkey):
            if len(ssq_aps) == 2:
                op("dve", lambda e: e.tensor_tensor(out=dst, in0=ssq_aps[0], in1=ssq_aps[1], op=ALU.add),
                   r=rkeys, w=[wkey])
                src = dst
                rk = [wkey]
            else:
                src = ssq_aps[0]
                rk = rkeys
            op("act", lambda e: e.activation(out=dst, in_=src, func=AF.Sqrt, scale=1.0 / D, bias=stat[:, 63:64]),
               r=rk + ["epsc"], w=[wkey])
            op("dve", lambda e: e.reciprocal(out=dst, in_=dst), r=[wkey], w=[wkey])

        def prenorm_T(xt, xkey, sub, dstT, col0, dkey, B):
            junk, xn = B["junk"], B["xn"]
            k = cnt["st"] % 16
            cnt["st"] += 1
            ssq = stat[:, 16 + k:17 + k]
            rs = stat[:, 32 + k:33 + k]
            sk, rk = "ssq%d" % k, "rs%d" % k
            op("pool", lambda e: e.memset(ssq, 0.0), w=[sk])
            op("act", lambda e: e.activation(out=junk[:, :], in_=xt, func=AF.Square, accum_out=ssq),
               r=[xkey, sk], w=[sk, "junk"])
            rstd_from([ssq], rs, [sk], rk)
            b = cnt["xn"] % 2
            cnt["xn"] += 1
            op("act", lambda e: e.activation(out=xn[b][:, :], in_=xt, func=AF.Copy, scale=rs),
               r=[xkey, rk], w=["xn%d" % b])
            pb = 2 + (cnt["pst"] % 2)
            cnt["pst"] += 1
            pst = ps[pb][:, 0:512].bitcast(BF16)
            for kc in range(KC):
                op("pe", lambda e, kc=kc: e.transpose(out=pst[:, kc * 128:(kc + 1) * 128],
                                                      in_=xn[b][:, kc * 128:(kc + 1) * 128], identity=ident_b),
                   r=["xn%d" % b, "cstb"], w=[psk[pb]])
            for kc in range(KC):
                eng = "dve" if kc % 2 == 0 else "pool"
                if eng == "pool":
                    eng = "dve"
                op(eng, lambda e, kc=kc: e.tensor_scalar(
                    out=dstT[:, kc, col0:col0 + 128], in0=pst[:, kc * 128:(kc + 1) * 128],
                    scalar1=scale_c[:, sub * 8 + kc:sub * 8 + kc + 1],
                    scalar2=modc[:, (3 * sub) * 8 + kc:(3 * sub) * 8 + kc + 1],
                    op0=ALU.mult, op1=ALU.add), r=[psk[pb], "scale_c", "modc"], w=[dkey])

        def post_res(pa, pb_, sub, ti, x_src, x_dst, junk, tmpb, tmpk, xs, xob, xok, post_cb):
            k = cnt["st"] % 16
            cnt["st"] += 1
            sa, sb_, rs = stat[:, 16 + k:17 + k], stat[:, 64 + k:65 + k], stat[:, 32 + k:33 + k]
            ska, skb, rk = "ssq%d" % k, "ssqb%d" % k, "rs%d" % k
            op("pool", lambda e: e.memset(sa, 0.0), w=[ska])
            op("pool", lambda e: e.memset(sb_, 0.0), w=[skb])
            op("act", lambda e: e.activation(out=junk[:, 0:512], in_=ps[pa][:, :], func=AF.Square, accum_out=sa),
               r=[psk[pa], ska], w=[ska, "junk"])
            op("act", lambda e: e.activation(out=junk[:, 512:1024], in_=ps[pb_][:, :], func=AF.Square, accum_out=sb_),
               r=[psk[pb_], skb], w=[skb, "junk"])
            rstd_from([sa, sb_], rs, [ska, skb], rk)
            for half, pbank in ((0, pa), (1, pb_)):
                op("dve", lambda e, half=half, pbank=pbank: e.scalar_tensor_tensor(
                    out=tmpb[:, half * 512:(half + 1) * 512], in0=ps[pbank][:, :], scalar=rs,
                    in1=gp_row[sub][:, half * 512:(half + 1) * 512], op0=ALU.mult, op1=ALU.mult),
                   r=[psk[pbank], rk, "gp_row%d" % sub], w=[tmpk])
            s_ = cnt["xs"] % len(xs)
            cnt["xs"] += 1
            xsk = "xs%d" % s_
            op("sp", lambda e: e.dma_start(out=xs[s_][:, :], in_=x_src[ti * 128:(ti + 1) * 128, :]),
               r=["dram_x%d_%d" % (sub, ti)], w=[xsk], dma=xsk)
            op("pool", lambda e: e.tensor_tensor(out=xob[:, :], in0=xs[s_][:, :], in1=tmpb[:, :], op=ALU.add),
               r=[xsk, tmpk], w=[xok])
            op("sp", lambda e: e.dma_start(out=x_dst[ti * 128:(ti + 1) * 128, :], in_=xob[:, :]),
               r=[xok], w=["dram_x%d_%d" % (sub + 1, ti)], dma=xok)
            if post_cb is not None:
                post_cb(xob[:, :], xok, ti)

        def ffn(sub, fi, x_src, x_dst, post_cb, B):
            hT, actT, wdn, wgu, xs, junk, tmp, xo, sg = (B[k] for k in ("hT", "actT", "wdn", "wgu", "xs", "junk", "tmp", "xo", "sg"))
            wgu_v = w_gu[fi].rearrange("(kc p) n -> p kc n", p=128)
            wdn_v = w_dn[fi].rearrange("(c p) n -> p c n", p=128)
            for hf in range(2):
                if x_src is not None:
                    for i in range(8):
                        ti = hf * 8 + i
                        s = cnt["xs"] % 3
                        cnt["xs"] += 1
                        op("sp", lambda e, s=s, ti=ti: e.dma_start(out=xs[s][:, :], in_=x_src[ti * 128:(ti + 1) * 128, :]),
                           r=["dram_x%d_%d" % (sub, ti)], w=["xs%d" % s], dma="xs%d" % s)
                        prenorm_T(xs[s][:, :], "xs%d" % s, sub, hT, i * 128, "hT_%d" % i, B)
                if hf == 0:
                    for c in range(HC):
                        op("pool", lambda e, c=c: e.dma_start(out=wdn[:, c, :], in_=wdn_v[:, c, :]),
                           w=["wdn_%d" % c], dma="wdn%d" % (c % 4))
                for c in range(HC):
                    sl = (hf * HC + c) % 3
                    op("pool", lambda e, c=c, sl=sl: e.dma_start(out=wgu[sl][:, :, 0:128], in_=wgu_v[:, :, c * 128:(c + 1) * 128]),
                       w=["wgu%da" % sl], dma="wgu%da" % sl)
                    op("pool", lambda e, c=c, sl=sl: e.dma_start(out=wgu[sl][:, :, 128:256],
                                                                 in_=wgu_v[:, :, DFF + c * 128:DFF + (c + 1) * 128]),
                       w=["wgu%db" % sl], dma="wgu%db" % sl)
                    for tg in range(2):
                        bg, bu = 2 * tg, 2 * tg + 1
                        hk = ["hT_%d" % i for i in range(tg * 4, tg * 4 + 4)]
                        for kc in range(KC):
                            op("pe", lambda e, kc=kc, sl=sl, tg=tg, bg=bg: e.matmul(
                                ps[bg][:, :], lhsT=wgu[sl][:, kc, 0:128], rhs=hT[:, kc, tg * 512:(tg + 1) * 512],
                                start=(kc == 0), stop=(kc == KC - 1)), r=["wgu%da" % sl] + hk, w=[psk[bg]])
                        for kc in range(KC):
                            op("pe", lambda e, kc=kc, sl=sl, tg=tg, bu=bu: e.matmul(
                                ps[bu][:, :], lhsT=wgu[sl][:, kc, 128:256], rhs=hT[:, kc, tg * 512:(tg + 1) * 512],
                                start=(kc == 0), stop=(kc == KC - 1)), r=["wgu%db" % sl] + hk, w=[psk[bu]])
                        op("act", lambda e, tg=tg, bg=bg: e.activation(out=sg[tg][:, :], in_=ps[bg][:, :], func=AF.Silu),
                           r=[psk[bg]], w=["sg%d" % tg])
                        op("dve", lambda e, tg=tg, bu=bu, c=c: e.tensor_tensor(
                            out=actT[:, c, tg * 512:(tg + 1) * 512], in0=sg[tg][:, :], in1=ps[bu][:, :], op=ALU.mult),
                           r=["sg%d" % tg, psk[bu]], w=["actT_%d_%d" % (c, tg)])
                pending = None
                for i in range(8):
                    ti = hf * 8 + i
                    tg = i // 4
                    pa, pb_ = (4, 5) if i % 2 == 0 else (6, 7)
                    for half, pbank in ((0, pa), (1, pb_)):
                        for c in range(HC):
                            op("pe", lambda e, c=c, i=i, half=half, pbank=pbank: e.matmul(
                                ps[pbank][:, :], lhsT=actT[:, c, i * 128:(i + 1) * 128],
                                rhs=wdn[:, c, half * 512:(half + 1) * 512], start=(c == 0), stop=(c == HC - 1)),
                               r=["actT_%d_%d" % (c, tg), "wdn_%d" % c], w=[psk[pbank]])
                    if pending is not None:
                        post_cb(*pending)
                        pending = None
                    post_res(pa, pb_, sub, ti, x_src, x_dst, junk, tmp[i % 2], "tmp%d" % (i % 2), xs, xo[i % 2], "xo%d" % (i % 2), None)
                    if post_cb is not None:
                        pending = (xo[i % 2][:, :], "xo%d" % (i % 2), ti)
                if pending is not None:
                    post_cb(*pending)
                    pending = None

        def post1(xt, xkey, ti):
            prenorm_T(xt, xkey, 1, h2T, ti * 128, "h2T_%d" % ti, B)

        S.muted = "ffn1" in self.skip
        ffn(0, 0, x, x1d, post1, B)
        S.muted = False
        if "ffn1" in self.skip:
            op("pool", lambda e: e.memset(h2T[:, :, :], 0.01), w=["h2T_%d" % i for i in range(NT)])
        if self.stop_after == "ffn1":
            dbg = self.nc.dram_tensor("dbg_h2T", [128, KC * NTOK], BF16, kind="ExternalOutput").ap()
            self.outs.append("dbg_h2T")
            op("sp", lambda e: e.dma_start(out=dbg, in_=h2T[:, :, :].rearrange("p k n -> p (k n)")),
               r=["h2T_%d" % i for i in range(NT)], dma="dbg")
            return
        S.barrier()

        RG = [[0, 1, 2, 3], [4, 5, 6, 7]]
        kT_src = self.dscratch("kT_src", [8 * 128, NTOK], BF16)
        kT_g = [self.dscratch("kT_g%d" % q, [4 * 256, NTOK], BF16) for q in range(4)]
        v_src = self.dscratch("v_src", [NTOK, 1024], BF16)
        v_g = [self.dscratch("v_g%d" % q, [4 * 512, 1024], BF16) for q in range(4)]
        fl_src = self.dscratch("fl_src", [128, 256], F32)
        fl_g = self.dscratch("fl_g", [4 * 128, 256], F32)
        gl_src = self.dscratch("gl_src", [128, 1536], F32)
        gl_g = self.dscratch("gl_g", [4 * 128, 1536], F32)
        win_v = w_in.rearrange("(kc p) n -> p kc n", p=128)
        h2k = lambda t0, t1: ["h2T_%d" % i for i in range(t0, t1)]
        onec = stat[:, 62:63]

        S.muted = "fox" in self.skip
        self.off = mark
        obT = self.sb("obT", [128, 8, NTOK], BF16)
        lf = self.sb("lf", [128, 256], F32)
        lfa = self.sb("lfa", [128, 256], F32)
        floc = self.sb("floc", [128, 256], F32)
        cq = self.sb("cq", [128, 64], F32)
        rall = self.sb("rall", [128, 256], F32)
        bfrow = self.sb("bfrow", [1, 16], F32)
        bfrep = self.sb("bfrep", [128, 16], F32)
        oaT = self.sb("oaT", [128, 8, NTOK], BF16)
        mark2 = self.off
        wk = [self.sb("wk%d" % i, [128, KC, 128], BF16) for i in range(2)]
        wv = self.sb("wv", [128, KC, 1024], BF16)
        wf = self.sb("wf", [128, KC, 16], BF16)
        ksb = [self.sb("ksb%d" % i, [128, NTOK], BF16) for i in range(2)]
        vsb = [self.sb("vsb%d" % i, [128, 1024], BF16) for i in range(2)]

        op("sp", lambda e: e.dma_start(out=bfrow[:, :], in_=b_f), w=["bfrow"], dma="bfrow")
        op("pe", lambda e: e.matmul(ps[5][:, 0:16], lhsT=ones_f[0:1, :], rhs=bfrow[0:1, :], start=True, stop=True),
           r=["bfrow", "cst"], w=[psk[5]])
        op("dve", lambda e: e.tensor_copy(out=bfrep[:, :], in_=ps[5][:, 0:16]), r=[psk[5]], w=["bfrep"])
        ev = [0]

        def evac(out_ap, in_ap, rk, wk_):
            ev[0] += 1
            if ev[0] % 2 == 0:
                op("act", lambda e: e.copy(out=out_ap, in_=in_ap), r=rk, w=wk_)
            else:
                op("dve", lambda e: e.tensor_copy(out=out_ap, in_=in_ap), r=rk, w=wk_)

        op("pool", lambda e: e.dma_start(out=wv[:, :, :], in_=win_v[:, :, O_VB:O_VB + 1024]), w=["wv"], dma="wv")
        op("pool", lambda e: e.dma_start(out=wf[:, :, :], in_=win_v[:, :, O_FB:O_FB + 16]), w=["wf"], dma="wf")
        for hp in range(8):
            sl = hp % 2
            op("pool", lambda e, hp=hp, sl=sl: e.dma_start(out=wk[sl][:, :, :], in_=win_v[:, :, O_KB + hp * 128:O_KB + (hp + 1) * 128]),
               w=["wk%d" % sl], dma="wk%d" % sl)
            for tg in range(4):
                bank = tg % 2
                for kc in range(KC):
                    op("pe", lambda e, kc=kc, sl=sl, tg=tg, bank=bank: e.matmul(
                        ps[bank][:, :], lhsT=wk[sl][:, kc, :], rhs=h2T[:, kc, tg * 512:(tg + 1) * 512],
                        start=(kc == 0), stop=(kc == KC - 1)), r=["wk%d" % sl] + h2k(tg * 4, tg * 4 + 4), w=[psk[bank]])
                evac(ksb[sl][:, tg * 512:(tg + 1) * 512], ps[bank][:, :], [psk[bank]], ["ksb%d_%d" % (sl, tg)])
            op("sp", lambda e, hp=hp, sl=sl: e.dma_start(out=kT_src[hp * 128:(hp + 1) * 128, :], in_=ksb[sl][:, :]),
               r=["ksb%d_%d" % (sl, tg) for tg in range(4)], w=["kTsrc_%d" % hp], dma="ksrc%d" % sl)
        for blk in range(16):
            b = blk % 2
            for half in range(2):
                bank = 2 + half
                for kc in range(KC):
                    op("pe", lambda e, kc=kc, blk=blk, half=half, bank=bank: e.matmul(
                        ps[bank][:, :], lhsT=h2T[:, kc, blk * 128:(blk + 1) * 128], rhs=wv[:, kc, half * 512:(half + 1) * 512],
                        start=(kc == 0), stop=(kc == KC - 1)), r=["wv", "h2T_%d" % blk], w=[psk[bank]])
                evac(vsb[b][:, half * 512:(half + 1) * 512], ps[bank][:, :], [psk[bank]], ["vsb%d_%d" % (b, half)])
            for kc in range(KC):
                op("pe", lambda e, kc=kc, blk=blk: e.matmul(
                    ps[4][:, blk * 16:(blk + 1) * 16], lhsT=h2T[:, kc, blk * 128:(blk + 1) * 128], rhs=wf[:, kc, :],
                    start=(kc == 0), stop=(kc == KC - 1)), r=["wf", "h2T_%d" % blk], w=[psk[4]])
            op("sp", lambda e, blk=blk, b=b: e.dma_start(out=v_src[blk * 128:(blk + 1) * 128, :], in_=vsb[b][:, :]),
               r=["vsb%d_0" % b, "vsb%d_1" % b], w=["vsrc_%d" % blk], dma="vsrc%d" % b)
        for blk in range(16):
            op("dve", lambda e, blk=blk: e.tensor_tensor(out=lf[:, blk * 16:(blk + 1) * 16], in0=ps[4][:, blk * 16:(blk + 1) * 16],
                                                        in1=bfrep[:, :], op=ALU.add), r=[psk[4], "bfrep"], w=["lf"])
        op("act", lambda e: e.activation(out=lfa[:, :], in_=lf[:, :], func=AF.Abs), r=["lf"], w=["lfa"])
        op("act", lambda e: e.activation(out=lfa[:, :], in_=lfa[:, :], func=AF.Exp, scale=-1.0), r=["lfa"], w=["lfa"])
        op("act", lambda e: e.activation(out=lfa[:, :], in_=lfa[:, :], func=AF.Ln, bias=onec, scale=1.0), r=["lfa", "onec"], w=["lfa"])
        op("dve", lambda e: e.tensor_single_scalar(out=lf[:, :], in_=lf[:, :], scalar=0.0, op=ALU.min), r=["lf"], w=["lf"])
        op("dve", lambda e: e.tensor_tensor(out=lf[:, :], in0=lf[:, :], in1=lfa[:, :], op=ALU.subtract), r=["lf", "lfa"], w=["lf"])
        for blk in range(16):
            lst = [(ones_f, b2) for b2 in range(blk)] + [(U_f, blk)]
            for i, (lh, b2) in enumerate(lst):
                op("pe", lambda e, lh=lh, b2=b2, blk=blk, i=i, n=len(lst): e.matmul(
                    ps[5][:, blk * 16:(blk + 1) * 16], lhsT=lh, rhs=lf[:, b2 * 16:(b2 + 1) * 16],
                    start=(i == 0), stop=(i == n - 1)), r=["lf", "cst"], w=[psk[5]])
        op("dve", lambda e: e.tensor_copy(out=floc[:, :], in_=ps[5][:, 0:256]), r=[psk[5]], w=["floc"])
        op("sp", lambda e: e.dma_start(out=fl_src, in_=floc[:, :]), r=["floc"], w=["flsrc"], dma="flsrc")
        op("pool", lambda e: e.memset(cq[:, 0:16], 0.0), w=["cq0"])
        for qt in range(1, 4):
            for i in range(4 * qt):
                op("pe", lambda e, qt=qt, i=i: e.matmul(ps[6][:, qt * 16:(qt + 1) * 16], lhsT=ones_f, rhs=lf[:, i * 16:(i + 1) * 16],
                                                       start=(i == 0), stop=(i == 4 * qt - 1)), r=["lf", "cst"], w=[psk[6]])
        op("dve", lambda e: e.tensor_copy(out=cq[:, 16:64], in_=ps[6][:, 16:64]), r=[psk[6]], w=["cq1"])
        for blk in range(16):
            q_ = blk // 4
            op("dve", lambda e, blk=blk, q_=q_: e.tensor_tensor(out=rall[:, blk * 16:(blk + 1) * 16], in0=floc[:, blk * 16:(blk + 1) * 16],
                                                               in1=cq[:, q_ * 16:(q_ + 1) * 16], op=ALU.subtract),
               r=["floc", "cq0", "cq1"], w=["rall"])
        if self.stop_after == "m1a":
            dbg = self.nc.dram_tensor("dbg_lf", [128, 768], F32, kind="ExternalOutput").ap()
            self.outs.append("dbg_lf")
            op("sp", lambda e: e.dma_start(out=dbg[:, 0:256], in_=lf[:, :]), r=["lf"], dma="dbg")
            op("sp", lambda e: e.dma_start(out=dbg[:, 256:512], in_=floc[:, :]), r=["floc"], dma="dbg")
            op("sp", lambda e: e.dma_start(out=dbg[:, 512:768], in_=rall[:, :]), r=["rall"], dma="dbg")
            return
        for q in range(4):
            op("pool", lambda e, q=q: e.collective_compute("AllGather", ALU.bypass, replica_groups=RG,
                                                           ins=[kT_src[q * 256:(q + 1) * 256, :]], outs=[kT_g[q]]),
               r=["kTsrc_%d" % i for i in (2 * q, 2 * q + 1)], w=["kTg%d" % q], dma="ag_k%d" % q, inc=1)
        for q in range(4):
            op("pool", lambda e, q=q: e.collective_compute("AllGather", ALU.bypass, replica_groups=RG,
                                                           ins=[v_src[q * 512:(q + 1) * 512, :]], outs=[v_g[q]]),
               r=["vsrc_%d" % i for i in range(16)], w=["vg%d" % q], dma="ag_v%d" % q, inc=1)
        op("pool", lambda e: e.collective_compute("AllGather", ALU.bypass, replica_groups=RG, ins=[fl_src], outs=[fl_g]),
           r=["flsrc"], w=["flg_d"], dma="ag_f", inc=1)
        if self.stop_after == "m1":
            dbg = self.nc.dram_tensor("dbg_lf", [128, 768], F32, kind="ExternalOutput").ap()
            self.outs.append("dbg_lf")
            op("sp", lambda e: e.dma_start(out=dbg[:, 0:256], in_=lf[:, :]), r=["lf"], dma="dbg")
            op("sp", lambda e: e.dma_start(out=dbg[:, 256:512], in_=floc[:, :]), r=["floc"], dma="dbg")
            op("sp", lambda e: e.dma_start(out=dbg[:, 512:768], in_=rall[:, :]), r=["rall"], dma="dbg")
            return
        S.barrier(skip_prefix="ag_")

        S.muted = False
        self.off = mark2
        mark3 = self.off
        wal = self.sb("wal", [128, KC, 16], BF16)
        alT = self.sb("alT", [128, NTOK], F32)
        wa2 = self.sb("wa2", [128, 512], F32)
        bac = self.sb("bac", [128, 4], F32)
        ggrow1 = self.sb("ggrow1", [1, D], F32)
        ggrow = self.sb("ggrow", [128, D], F32)
        wqa = self.sb("wqa", [128, KC, 128], BF16)
        wka = self.sb("wka", [128, KC, 128], BF16)
        wva = self.sb("wva", [128, KC, 256], BF16)
        wra = self.sb("wra", [128, KC, 256], BF16)
        qT = self.sb("qT", [128, NTOK], BF16)
        kT = self.sb("kT", [128, NTOK], BF16)
        va = self.sb("va", [128, 16, 256], BF16)
        bT = self.sb("bT", [128, NTOK], F32)
        tA = self.sb("tA", [128, 512], F32)
        tM = self.sb("tM", [128, 512], F32)
        Sf = self.sb("Sf", [128, 256], F32)
        Sst = self.sb("Sst", [128, 4, 256], F32)
        NB = 3
        Sb = [self.sb("Sb%d" % i, [128, 256], BF16) for i in range(NB)]
        e1 = [self.sb("e1_%d" % i, [128, 128], F32) for i in range(NB)]
        e2 = [self.sb("e2_%d" % i, [128, 128], F32) for i in range(NB)]
        kht = [self.sb("kht%d" % i, [128, 128], BF16) for i in range(NB)]
        kh_all = self.sb("kh_all", [128, 16, 128], BF16)
        At_all = self.sb("At_all", [128, 16, 128], BF16)
        dcolT = self.sb("dcolT", [128, 16], F32)
        sr = [self.sb("sr%d" % i, [128, 256], F32) for i in range(2)]
        osb = [self.sb("osb%d" % i, [128, 256], F32) for i in range(2)]
        oab = [self.sb("oab%d" % i, [128, 256], BF16) for i in range(2)]
        gjunk = self.sb("gjunk", [128, 256], BF16)
        gsq = self.sb("gsq", [128, 48], F32)
        glsrc = self.sb("glsrc", [128, 1536], F32)
        glg = glsrc[:, :]
        gst = self.sb("gst", [128, 8], F32)

        op("pool", lambda e: e.dma_start(out=wal[:, :, :], in_=win_v[:, :, O_AL:O_AL + 16]), w=["wal"], dma="wal")
        op("pool", lambda e: e.memset(wa2[:, :], 0.0), w=["wa2"])
        op("pool", lambda e: e.memset(alT[:, :], 0.0), w=["alT"])
        op("sp", lambda e: e.dma_start(out=wa2[0:16, :], in_=w_a2), w=["wa2"], dma="wa2")
        op("sp", lambda e: e.dma_start(out=ggrow1[:, :], in_=g_gla), w=["ggrow1"], dma="ggrow1")
        vec_to_cols(b_a, 4, bac[:, :], "bac", 7)
        for half in range(2):
            op("pe", lambda e, half=half: e.matmul(ps[half][:, :], lhsT=ones_f[0:1, :], rhs=ggrow1[0:1, half * 512:(half + 1) * 512],
                                                  start=True, stop=True), r=["ggrow1", "cst"], w=[psk[half]])
            op("dve", lambda e, half=half: e.tensor_copy(out=ggrow[:, half * 512:(half + 1) * 512], in_=ps[half][:, :]),
               r=[psk[half]], w=["ggrow"])
        for tg in range(4):
            for kc in range(KC):
                op("pe", lambda e, kc=kc, tg=tg: e.matmul(ps[2][0:16, :], lhsT=wal[:, kc, :], rhs=h2T[:, kc, tg * 512:(tg + 1) * 512],
                                                         start=(kc == 0), stop=(kc == KC - 1)), r=["wal"] + h2k(tg * 4, tg * 4 + 4), w=[psk[2]])
            op("dve", lambda e, tg=tg: e.tensor_copy(out=alT[0:16, tg * 512:(tg + 1) * 512], in_=ps[2][0:16, :]), r=[psk[2]], w=["alT"])

        if self.stop_after == "glaP":
            dbg = self.nc.dram_tensor("dbg_sst", [128, 1024], F32, kind="ExternalOutput").ap()
            self.outs.append("dbg_sst")
            op("sp", lambda e: e.dma_start(out=dbg, in_=ggrow[:, :]), r=["ggrow", "alT", "bac"], dma="dbg")
            return

        def gla_head(hh, full, part="both"):
            S.muted = (part == "loop")
            op("pool", lambda e: e.dma_start(out=wka[:, :, :], in_=win_v[:, :, O_KA + hh * 128:O_KA + (hh + 1) * 128]), w=["wka"], dma="wka")
            op("pool", lambda e: e.dma_start(out=wva[:, :, :], in_=win_v[:, :, O_VA + hh * 256:O_VA + (hh + 1) * 256]), w=["wva"], dma="wva")
            if full:
                op("pool", lambda e: e.dma_start(out=wqa[:, :, :], in_=win_v[:, :, O_QA + hh * 128:O_QA + (hh + 1) * 128]), w=["wqa"], dma="wqa")
                op("pool", lambda e: e.dma_start(out=wra[:, :, :], in_=win_v[:, :, O_RA + hh * 256:O_RA + (hh + 1) * 256]), w=["wra"], dma="wra")
            bcol = bac[:, hh:hh + 1]
            def pre(c):
                b = c % NB
                cs = slice(c * 128, (c + 1) * 128)
                tg = c // 4
                op("act", lambda e, b=b, cs=cs: e.activation(out=e2[b][:, :], in_=bT[:, cs], func=AF.Exp, scale=-1.0 / 16), r=["bT_%d" % c], w=["e2_%d" % b])
                op("act", lambda e, b=b, cs=cs: e.activation(out=e1[b][:, :], in_=bT[:, cs], func=AF.Exp, scale=1.0 / 16), r=["bT_%d" % c], w=["e1_%d" % b])
                op("dve", lambda e, b=b, cs=cs: e.tensor_tensor(out=kT[:, cs], in0=kT[:, cs], in1=e2[b][:, :], op=ALU.mult),
                   r=["kT_%d" % tg, "e2_%d" % b], w=["kTc_%d" % c])
                op("dve", lambda e, b=b, cs=cs: e.tensor_scalar(out=kht[b][:, :], in0=kT[:, cs], scalar1=e1[b][:, 127:128], scalar2=None, op0=ALU.mult),
                   r=["kTc_%d" % c, "e1_%d" % b], w=["kht%d" % b])
                op("dve", lambda e, b=b, c=c: e.tensor_copy(out=dcolT[:, c:c + 1], in_=e1[b][:, 127:128]), r=["e1_%d" % b], w=["dcol_%d" % c])
                p3 = ps[3][:, 0:64].bitcast(BF16)
                op("pe", lambda e, b=b, p3=p3: e.transpose(out=p3, in_=kht[b][:, :], identity=ident_b), r=["kht%d" % b, "cstb"], w=[psk[3]])
                op("dve", lambda e, c=c, p3=p3: e.tensor_copy(out=kh_all[:, c, :], in_=p3), r=[psk[3]], w=["kh_%d" % c])
                if full:
                    op("pool", lambda e, b=b, cs=cs: e.tensor_tensor(out=qT[:, cs], in0=qT[:, cs], in1=e1[b][:, :], op=ALU.mult),
                       r=["qT_%d" % tg, "e1_%d" % b], w=["qTc_%d" % c])
                    op("pe", lambda e, cs=cs: e.matmul(ps[4][:, 0:128], lhsT=kT[:, cs], rhs=qT[:, cs], start=True, stop=True),
                       r=["kTc_%d" % c, "qTc_%d" % c], w=[psk[4]])
                    op("dve", lambda e, c=c: e.tensor_tensor(out=At_all[:, c, :], in0=ps[4][:, 0:128], in1=U_f, op=ALU.mult), r=[psk[4], "cst"], w=["At_%d" % c])

            for tg in range(4):
                cs = slice(tg * 512, (tg + 1) * 512)
                op("pe", lambda e, cs=cs: e.matmul(ps[0][:, :], lhsT=wa2[:, hh * 128:(hh + 1) * 128], rhs=alT[:, cs], start=True, stop=True),
                   r=["wa2", "alT"], w=[psk[0]])
                op("act", lambda e: e.activation(out=tM[:, :], in_=ps[0][:, :], func=AF.Identity, bias=bcol, scale=1.0), r=[psk[0], "bac"], w=["tM"])
                op("act", lambda e: e.activation(out=tA[:, :], in_=tM[:, :], func=AF.Abs), r=["tM"], w=["tA"])
                op("act", lambda e: e.activation(out=tA[:, :], in_=tA[:, :], func=AF.Exp, scale=-1.0), r=["tA"], w=["tA"])
                op("act", lambda e: e.activation(out=tA[:, :], in_=tA[:, :], func=AF.Ln, bias=onec, scale=1.0), r=["tA", "onec"], w=["tA"])
                op("dve", lambda e: e.tensor_scalar(out=tM[:, :], in0=tM[:, :], scalar1=0.0, scalar2=None, op0=ALU.min),
                   r=["tM", "tA"], w=["tM"])
                op("dve", lambda e: e.tensor_tensor(out=tA[:, :], in0=tM[:, :], in1=tA[:, :], op=ALU.subtract), r=["tM", "tA"], w=["tA"])
                for j in range(4):
                    c = tg * 4 + j
                    op("dve", lambda e, j=j, c=c: e.tensor_tensor_scan(out=bT[:, c * 128:(c + 1) * 128], data0=ones_f, data1=tA[:, j * 128:(j + 1) * 128],
                                                                    initial=0.0, op0=ALU.mult, op1=ALU.add), r=["tA", "cst"], w=["bT_%d" % c])
                for kc in range(KC):
                    op("pe", lambda e, kc=kc, cs=cs: e.matmul(ps[1][:, :], lhsT=wka[:, kc, :], rhs=h2T[:, kc, cs], start=(kc == 0), stop=(kc == KC - 1)),
                       r=["wka"] + h2k(tg * 4, tg * 4 + 4), w=[psk[1]])
                op("act", lambda e, cs=cs: e.copy(out=kT[:, cs], in_=ps[1][:, :]), r=[psk[1]], w=["kT_%d" % tg] + ["kTc_%d" % (tg * 4 + j_) for j_ in range(4)])
                if full:
                    for kc in range(KC):
                        op("pe", lambda e, kc=kc, cs=cs: e.matmul(ps[1][:, :], lhsT=wqa[:, kc, :], rhs=h2T[:, kc, cs], start=(kc == 0), stop=(kc == KC - 1)),
                           r=["wqa"] + h2k(tg * 4, tg * 4 + 4), w=[psk[1]])
                    op("dve", lambda e, cs=cs: e.tensor_scalar(out=qT[:, cs], in0=ps[1][:, :], scalar1=128.0 ** -0.5, scalar2=None, op0=ALU.mult),
                       r=[psk[1]], w=["qT_%d" % tg] + ["qTc_%d" % (tg * 4 + j_) for j_ in range(4)])
                for j_ in range(4):
                    pre(tg * 4 + j_)
            for blk in range(16):
                for kc in range(KC):
                    op("pe", lambda e, kc=kc, blk=blk: e.matmul(ps[2][:, 0:256], lhsT=h2T[:, kc, blk * 128:(blk + 1) * 128], rhs=wva[:, kc, :],
                                                               start=(kc == 0), stop=(kc == KC - 1)), r=["wva", "h2T_%d" % blk], w=[psk[2]])
                evac(va[:, blk, :], ps[2][:, 0:256], [psk[2]], ["va_%d" % blk])
            S.muted = (part == "loop")
            PRECOMPUTE_MARK = None

            if full:
                op("pool", lambda e: e.memset(gsq[:, :], 0.0), w=["gsq"])

            def kvmm(c):
                kvb = c % 2
                op("pe", lambda e: e.matmul(ps[kvb][:, 0:256], lhsT=kh_all[:, c, :], rhs=va[:, c, :], start=True, stop=True),
                   r=["kh_%d" % c, "va_%d" % c], w=[psk[kvb]])

            def rmm(c):
                rb = 2 + 2 * (c % 3)
                cs = slice(c * 128, (c + 1) * 128)
                for kc in range(KC):
                    op("pe", lambda e, kc=kc: e.matmul(ps[rb][:, 0:256], lhsT=h2T[:, kc, cs], rhs=wra[:, kc, :], start=(kc == 0), stop=(kc == KC - 1)),
                       r=["wra", "h2T_%d" % c], w=[psk[rb]])

            def state(c):
                b = c % NB
                ob = 5 if c % 2 == 0 else 7
                cs = slice(c * 128, (c + 1) * 128)
                if full:
                    op("pool", lambda e: e.tensor_copy(out=Sb[b][:, :], in_=Sf[:, :]), r=["Sf"], w=["Sb%d" % b])
                    op("pe", lambda e: e.matmul(ps[ob][:, 0:256], lhsT=qT[:, cs], rhs=Sb[b][:, :], start=True, stop=False), r=["qTc_%d" % c, "Sb%d" % b], w=[psk[ob]])
                    op("pe", lambda e: e.matmul(ps[ob][:, 0:256], lhsT=At_all[:, c, :], rhs=va[:, c, :], start=False, stop=True), r=["At_%d" % c, "va_%d" % c], w=[psk[ob]])
                kvb = c % 2
                op("dve", lambda e: e.scalar_tensor_tensor(out=Sf[:, :], in0=Sf[:, :], scalar=dcolT[:, c:c + 1], in1=ps[kvb][:, 0:256],
                                                           op0=ALU.mult, op1=ALU.add), r=["Sf", "dcol_%d" % c, psk[kvb]], w=["Sf"])

            def post_a(c):
                ob = 5 if c % 2 == 0 else 7
                rb = 2 + 2 * (c % 3)
                j = c % 2
                op("act", lambda e: e.activation(out=sr[j][:, :], in_=ps[rb][:, 0:256], func=AF.Exp, scale=-1.0), r=[psk[rb]], w=["sr%d" % j])
                op("dve", lambda e: e.tensor_scalar(out=sr[j][:, :], in0=sr[j][:, :], scalar1=1.0, scalar2=None, op0=ALU.add), r=["sr%d" % j], w=["sr%d" % j])
                op("dve", lambda e: e.reciprocal(out=sr[j][:, :], in_=sr[j][:, :]), r=["sr%d" % j], w=["sr%d" % j])
                op("dve", lambda e: e.tensor_tensor(out=sr[j][:, :], in0=sr[j][:, :], in1=ps[rb][:, 0:256], op=ALU.mult), r=["sr%d" % j, psk[rb]], w=["sr%d" % j])
                op("act", lambda e: e.activation(out=gjunk[:, :], in_=ps[ob][:, 0:256], func=AF.Square, accum_out=gsq[:, c:c + 1]),
                   r=[psk[ob], "gsq"], w=["gsqa_%d" % c, "gjunk"])
                op("act", lambda e: e.activation(out=gsq[:, 16 + c:17 + c], in_=gsq[:, c:c + 1], func=AF.Ln, scale=1.0 / 256, bias=stat[:, 63:64]),
                   r=["gsqa_%d" % c, "epsc"], w=["gsqb_%d" % c])
                op("act", lambda e: e.activation(out=gsq[:, 32 + c:33 + c], in_=gsq[:, 16 + c:17 + c], func=AF.Exp, scale=-0.5), r=["gsqb_%d" % c], w=["gsqc_%d" % c])
                op("dve", lambda e: e.scalar_tensor_tensor(out=osb[j][:, :], in0=ps[ob][:, 0:256], scalar=gsq[:, 32 + c:33 + c],
                                                           in1=ggrow[:, hh * 256:(hh + 1) * 256], op0=ALU.mult, op1=ALU.mult),
                   r=[psk[ob], "gsqc_%d" % c, "ggrow"], w=["osb%d" % j])
                op("dve", lambda e: e.tensor_tensor(out=oab[j][:, :], in0=osb[j][:, :], in1=sr[j][:, :], op=ALU.mult), r=["osb%d" % j, "sr%d" % j], w=["oab%d" % j])

            def post_b(c):
                j = c % 2
                cs = slice(c * 128, (c + 1) * 128)
                p3 = ps[3][:, 0:128].bitcast(BF16)
                for i in range(2):
                    op("pe", lambda e, i=i: e.transpose(out=p3[:, i * 128:(i + 1) * 128], in_=oab[j][:, i * 128:(i + 1) * 128], identity=ident_b),
                       r=["oab%d" % j, "cstb"], w=[psk[3]])
                op("act", lambda e: e.copy(out=oaT[:, 2 * hh:2 * hh + 2, cs], in_=p3.rearrange("p (a b) -> p a b", a=2)),
                   r=[psk[3]], w=["oaT_%d" % c])

            S.muted = (part == "setup")
            if full:
                op("dve", lambda e: e.tensor_copy(out=Sf[:, :], in_=Sst[:, hh, :]), r=["Sst"], w=["Sf"])
            else:
                op("pool", lambda e: e.memset(Sf[:, :], 0.0), w=["Sf"])
            kvmm(0)
            if full:
                rmm(0)
            for c in range(16):
                state(c)
                if c + 1 < 16:
                    kvmm(c + 1)
                    if full:
                        rmm(c + 1)
                if full and c >= 1:
                    post_a(c - 1)
                if full and c >= 2:
                    post_b(c - 2)
            if full:
                post_a(15)
                post_b(14)
                post_b(15)
            if not full:
                op("dve", lambda e: e.tensor_copy(out=glsrc[:, hh * 257:hh * 257 + 256], in_=Sf[:, :]), r=["Sf"], w=["glsrc"])
                op("dve", lambda e: e.reduce_sum(out=gst[:, 4:5], in_=bT[:, :].rearrange("p (c t) -> p c t", t=128)[:, :, 127],
                                                 axis=mybir.AxisListType.X), r=["bT_%d" % c for c in range(16)], w=["gst4"])
                op("act", lambda e: e.activation(out=glsrc[:, hh * 257 + 256:hh * 257 + 257], in_=gst[:, 4:5], func=AF.Exp, scale=1.0 / 16),
                   r=["gst4"], w=["glsrc"])

        op("pool", lambda e: e.memset(glsrc[:, :], 0.0), w=["glsrc"])
        for hh in range(4):
            gla_head(hh, False)
            S.muted = False
        if self.stop_after == "glaH":
            dbg = self.nc.dram_tensor("dbg_sst", [128, 1024], F32, kind="ExternalOutput").ap()
            self.outs.append("dbg_sst")
            op("sp", lambda e: e.dma_start(out=dbg, in_=glsrc[:, 0:1024]), r=["glsrc"], dma="dbg")
            return
        op("sp", lambda e: e.dma_start(out=gl_src, in_=glsrc[:, :]), r=["glsrc"], w=["glsrc_d"], dma="glsrc_d")
        op("pool", lambda e: e.collective_compute("AllGather", ALU.bypass, replica_groups=RG, ins=[gl_src], outs=[gl_g]),
           r=["glsrc_d"], w=["glg_d"], dma="ag_g", inc=1)
        gla_head(0, True, "setup")
        S.muted = False
        op("pool", lambda e: e.memset(Sst[:, :, :], 0.0), w=["Sst"])
        for p in range(4):
            op("sp", lambda e, p=p: e.dma_start(out=glg, in_=gl_g[p * 128:(p + 1) * 128, :]), r=["glg_d"],
               w=["glg", "glsrc"], dma="glg")
            for hh in range(4):
                dcol = glg[:, hh * 257 + 256:hh * 257 + 257]
                vcol = metat[:, p:p + 1]
                op("dve", lambda e, dcol=dcol, vcol=vcol: e.tensor_scalar(out=gst[:, 5:6], in0=dcol, scalar1=-1.0, scalar2=vcol, op0=ALU.add, op1=ALU.mult),
                   r=["glg", "meta"], w=["gst5"])
                op("dve", lambda e: e.tensor_scalar(out=gst[:, 5:6], in0=gst[:, 5:6], scalar1=1.0, scalar2=None, op0=ALU.add), r=["gst5"], w=["gst5"])
                op("dve", lambda e, hh=hh: e.tensor_scalar(out=Sst[:, hh, :], in0=Sst[:, hh, :], scalar1=gst[:, 5:6], scalar2=None, op0=ALU.mult),
                   r=["Sst", "gst5"], w=["Sst"])
                op("dve", lambda e, hh=hh, vcol=vcol: e.scalar_tensor_tensor(out=Sst[:, hh, :], in0=glg[:, hh * 257:hh * 257 + 256], scalar=vcol,
                                                                        in1=Sst[:, hh, :], op0=ALU.mult, op1=ALU.add),
                   r=["glg", "meta", "Sst"], w=["Sst"])
        if self.stop_after == "glaA":
            dbg = self.nc.dram_tensor("dbg_sst", [128, 1024], F32, kind="ExternalOutput").ap()
            self.outs.append("dbg_sst")
            op("sp", lambda e: e.dma_start(out=dbg, in_=Sst[:, :, :].rearrange("p a b -> p (a b)")), r=["Sst"], dma="dbg")
            return
        for hh in range(4):
            gla_head(hh, True, "loop" if hh == 0 else "both")
            S.muted = False
        if self.stop_after == "gla":
            dbg = self.nc.dram_tensor("dbg_oaT", [128, 8 * NTOK], BF16, kind="ExternalOutput").ap()
            self.outs.append("dbg_oaT")
            op("sp", lambda e: e.dma_start(out=dbg, in_=oaT[:, :, :].rearrange("p k n -> p (k n)")),
               r=["oaT_%d" % a for a in range(16)], dma="dbg")
            return
        S.barrier()

        S.muted = "fox" in self.skip
        self.off = mark2
        Kaug = [self.sb("Kaug%d" % i, [128, 4 * NTOK], BF16) for i in range(2)]
        Vaug = [self.sb("Vaug%d" % i, [128, 64, 128], BF16) for i in range(2)]
        Qaug = [self.sb("Qaug%d" % i, [128, NTOK], BF16) for i in range(2)]
        pT = [self.sb("pT%d" % i, [128, 512], BF16) for i in range(4)]
        KB0 = self.sb("KB0", [128, 64, 16], F32)
        flg = self.sb("flg", [128, 4, 256], F32)
        totrep = self.sb("totrep", [128, 4, 16], F32)
        dp = self.sb("dp", [128, 4, 16], F32)
        fsel = self.sb("fsel", [128, 16, 64], F32)
        kbq = [self.sb("kbq%d" % i, [128, 64], F32) for i in range(2)]
        rcp = gp_row[0][:, 0:512]
        wq = [self.sb("wq%d" % i, [128, KC, 64], BF16) for i in range(2)]
        E127 = cst[:, 384:512]

        for sl in range(2):
            op("pool", lambda e, sl=sl: e.memset(Kaug[sl][0:64, :], 0.0), w=["KaugI%d" % sl])
            op("pool", lambda e, sl=sl: e.memset(Kaug[sl][0:1, :], 1.0), w=["KaugI%d" % sl])
            op("pool", lambda e, sl=sl: e.memset(Kaug[sl][32:33, :], 1.0), w=["KaugI%d" % sl])
            op("pool", lambda e, sl=sl: e.memset(Vaug[sl][:, :, 64:128], 1.0), w=["VaugI%d" % sl])
            op("pool", lambda e, sl=sl: e.memset(Qaug[sl][0:64, :], 0.0), w=["QaugI%d" % sl])
        op("pool", lambda e: e.memset(fsel[:, :, :], 0.0), w=["fsel"])
        op("sp", lambda e: e.dma_start(out=flg[:, :, :], in_=fl_g.rearrange("(s p) c -> p s c", p=128)), r=["flg_d"], w=["flg"], dma="flg")
        for p in range(4):
            op("pe", lambda e, p=p: e.matmul(ps[7][:, p * 16:(p + 1) * 16], lhsT=E127, rhs=flg[:, p, 240:256], start=True, stop=True),
               r=["flg", "cst"], w=[psk[7]])
        op("dve", lambda e: e.tensor_copy(out=totrep[:, :, :].rearrange("p a b -> p (a b)"), in_=ps[7][:, 0:64]), r=[psk[7]], w=["totrep"])
        for p in (3, 2, 1, 0):
            op("dve", lambda e, p=p: e.tensor_scalar(out=dp[:, p, :], in0=totrep[:, p, :], scalar1=metat[:, p:p + 1], scalar2=None, op0=ALU.mult),
               r=["totrep", "meta"], w=["dp%d" % p])
            if p < 3:
                op("dve", lambda e, p=p: e.tensor_tensor(out=dp[:, p, :], in0=dp[:, p, :], in1=dp[:, p + 1, :], op=ALU.add),
                   r=["dp%d" % p, "dp%d" % (p + 1)], w=["dp%d" % p])
        for p in range(3):
            for blk in range(16):
                op("dve", lambda e, p=p, blk=blk: e.scalar_tensor_tensor(
                    out=KB0[:, p * 16 + blk, :], in0=dp[:, p, :], scalar=metat[:, 4 + p:5 + p], in1=flg[:, p, blk * 16:(blk + 1) * 16],
                    op0=ALU.add, op1=ALU.subtract), r=["dp%d" % p_ for p_ in range(4)] + ["flg", "meta"], w=["KB0"])
        op("dve", lambda e: e.tensor_scalar(out=KB0[:, 48:64, :].rearrange("p a b -> p (a b)"), in0=floc[:, :], scalar1=-1.0, scalar2=None, op0=ALU.mult),
           r=["floc"], w=["KB0"])

        def fox_prep(h):
            sl, hp, hs = h % 2, h // 2, h % 2
            for p in range(3):
                r0 = p * 256 + (hp % 2) * 128 + hs * 64
                kq = hp // 2
                op("sp", lambda e, sl=sl, p=p, r0=r0, kq=kq: e.dma_start(out=Kaug[sl][64:128, p * NTOK:(p + 1) * NTOK], in_=kT_g[kq][r0:r0 + 64, :]),
                   r=["kTg%d" % kq], w=["Ka%d_%d" % (sl, p)], dma="ka%d_%d" % (sl, p))
                for vq in range(4):
                    op("sp", lambda e, sl=sl, p=p, vq=vq, h=h: e.dma_start(
                        out=Vaug[sl][:, p * 16 + 4 * vq:p * 16 + 4 * vq + 4, 0:64],
                        in_=v_g[vq][p * 512:(p + 1) * 512, h * 64:(h + 1) * 64].rearrange("(b p) d -> p b d", p=128)),
                       r=["vg%d" % vq, "VaugI%d" % sl], w=["Va%d_%d" % (sl, p)], dma="va%d_%d" % (sl, p))
            r0 = hp * 128 + hs * 64
            op("sp", lambda e, sl=sl, r0=r0: e.dma_start(out=Kaug[sl][64:128, 3 * NTOK:4 * NTOK], in_=kT_src[r0:r0 + 64, :]),
               r=["kTsrc_%d" % hp], w=["Ka%d_4" % sl], dma="ka%d_4" % sl)
            op("sp", lambda e, sl=sl, h=h: e.dma_start(out=Vaug[sl][:, 48:64, 0:64],
                                                      in_=v_src[:, h * 64:(h + 1) * 64].rearrange("(b p) d -> p b d", p=128)),
               r=["vsrc_%d" % i for i in range(16)] + ["VaugI%d" % sl], w=["Va%d_4" % sl], dma="va%d_4" % sl)
            op("pool", lambda e, sl=sl, h=h: e.dma_start(out=wq[sl][:, :, :], in_=win_v[:, :, O_QB + h * 64:O_QB + (h + 1) * 64]),
               w=["wq%d" % sl], dma="wq%d" % sl)
            for tg in range(4):
                for kc in range(KC):
                    op("pe", lambda e, kc=kc, sl=sl, tg=tg: e.matmul(
                        ps[5][64:128, :], lhsT=wq[sl][:, kc, :], rhs=h2T[:, kc, tg * 512:(tg + 1) * 512],
                        start=(kc == 0), stop=(kc == KC - 1)), r=["wq%d" % sl] + h2k(tg * 4, tg * 4 + 4), w=[psk[5]])
                op("dve", lambda e, sl=sl, tg=tg: e.tensor_scalar(out=Qaug[sl][64:128, tg * 512:(tg + 1) * 512], in0=ps[5][64:128, :],
                                                                 scalar1=0.125, scalar2=None, op0=ALU.mult),
                   r=[psk[5]], w=["Qq%d_%d" % (sl, tg)])
            rall_h = rall[:, :].rearrange("p (b h) -> p b h", h=16)[:, :, h]
            op("dve", lambda e, rall_h=rall_h: e.tensor_copy(out=fsel[:, :, 0], in_=rall_h), r=["rall"], w=["fsel"])
            op("dve", lambda e, rall_h=rall_h: e.tensor_copy(out=fsel[:, :, 32], in_=rall_h), r=["rall"], w=["fsel"])
            for qt in range(4):
                for i in range(4):
                    blk = 4 * qt + i
                    op("pe", lambda e, blk=blk, i=i: e.matmul(ps[6][0:64, i * 128:(i + 1) * 128], lhsT=fsel[:, blk, :], rhs=ident_f,
                                                              start=True, stop=True), r=["fsel", "cst"], w=[psk[6]])
                op("act", lambda e, sl=sl, qt=qt: e.copy(out=Qaug[sl][0:64, qt * 512:(qt + 1) * 512], in_=ps[6][0:64, :]),
                   r=[psk[6], "QaugI%d" % sl], w=["Qr%d_%d" % (sl, qt)])
                op("dve", lambda e, sl=sl, qt=qt: e.tensor_tensor(out=Qaug[sl][32:64, qt * 512:(qt + 1) * 512], in0=ps[6][32:64, :],
                                                                 in1=Qaug[sl][32:64, qt * 512:(qt + 1) * 512], op=ALU.subtract),
                   r=[psk[6], "Qr%d_%d" % (sl, qt)], w=["Qr%d_%d" % (sl, qt)])
            KB0_h = KB0[:, :, h]
            for qt in range(4):
                kb_ = gp_row[0][:, 512 + sl * 256 + qt * 64:512 + sl * 256 + (qt + 1) * 64]
                op("dve", lambda e, qt=qt, h=h, KB0_h=KB0_h, kb_=kb_: e.tensor_scalar(
                    out=kb_, in0=KB0_h, scalar1=cq[:, qt * 16 + h:qt * 16 + h + 1], scalar2=None, op0=ALU.add),
                   r=["KB0", "cq0", "cq1"], w=["kbq%d_%d" % (sl, qt)])

        def fox_attn(h, hook):
            sl, hp, hs = h % 2, h // 2, h % 2
            KB0_h = KB0[:, :, h]
            kbq4 = [gp_row[0][:, 512 + sl * 256 + q_ * 64:512 + sl * 256 + (q_ + 1) * 64] for q_ in range(4)]
            items = []
            for qt in range(4):
                blocks = [(p * 16 + b, False, 0, p) for p in range(3) for b in range(16)]
                blocks += [(48 + b, b >= 4 * qt, b - 4 * qt, 4) for b in range(4 * qt + 4)]
                for n, blk in enumerate(blocks):
                    items.append((qt, n, len(blocks)) + blk)
            DEPTH = 2
            nit = len(items)

            def emit_S(idx):
                qt, n, nb, bi, diag, i, p = items[idx]
                c0 = i * 128 if diag else 0
                sbk = 2 + idx % 3
                op("pe", lambda e, sl=sl, bi=bi, c0=c0, sbk=sbk, qt=qt: e.matmul(
                    ps[sbk][:, c0:512], lhsT=Kaug[sl][:, bi * 128:(bi + 1) * 128], rhs=Qaug[sl][:, qt * 512 + c0:(qt + 1) * 512],
                    start=True, stop=True),
                   r=["Ka%d_%d" % (sl, p if p < 4 else 4), "KaugI%d" % sl, "Qq%d_%d" % (sl, qt), "Qr%d_%d" % (sl, qt)], w=[psk[sbk]])

            for idx in range(min(DEPTH, nit)):
                emit_S(idx)
            for idx, (qt, n, nb, bi, diag, i, p) in enumerate(items):
                if idx == nit // 8 and hook is not None:
                    hook()
                c0 = i * 128 if diag else 0
                sbk = 2 + idx % 3
                pb = idx % 4
                acc = qt % 2
                op("act", lambda e, sbk=sbk, pb=pb, c0=c0, qt=qt, bi=bi: e.activation(
                    out=pT[pb][:, c0:512], in_=ps[sbk][:, c0:512], func=AF.Exp, bias=kbq4[qt][:, bi:bi + 1], scale=1.0),
                   r=[psk[sbk], "kbq%d_%d" % (sl, qt)], w=["pT%d" % pb])
                if diag:
                    op("dve", lambda e, pb=pb, c0=c0: e.tensor_tensor(out=pT[pb][:, c0:c0 + 128], in0=pT[pb][:, c0:c0 + 128],
                                                                     in1=U_b, op=ALU.mult), r=["pT%d" % pb, "cstb"], w=["pT%d" % pb])
                if idx + DEPTH < nit:
                    emit_S(idx + DEPTH)
                op("pe", lambda e, sl=sl, bi=bi, c0=c0, pb=pb, acc=acc, n=n, nb=nb: e.matmul(
                    ps[acc][:, c0:512], lhsT=Vaug[sl][:, bi, :], rhs=pT[pb][:, c0:512], start=(n == 0), stop=(n == nb - 1)),
                   r=["Va%d_%d" % (sl, p if p < 4 else 4), "VaugI%d" % sl, "pT%d" % pb], w=[psk[acc]])
                if n == nb - 1:
                    op("dve", lambda e, acc=acc: e.reciprocal(out=rcp[0:64, :], in_=ps[acc][64:128, :]), r=[psk[acc]], w=["rcp"])
                    op("dve", lambda e, acc=acc, hs=hs, hp=hp, qt=qt: e.tensor_tensor(
                        out=obT[hs * 64:(hs + 1) * 64, hp, qt * 512:(qt + 1) * 512], in0=ps[acc][0:64, :], in1=rcp[0:64, :], op=ALU.mult),
                       r=[psk[acc], "rcp"], w=["obT_%d_%d" % (hp, qt)])
        fox_prep(0)
        for h in range(16):
            fox_attn(h, (lambda h=h: fox_prep(h + 1)) if h + 1 < 16 else None)
        if self.stop_after == "fox":
            dbg = self.nc.dram_tensor("dbg_obT", [128, 8 * NTOK], BF16, kind="ExternalOutput").ap()
            self.outs.append("dbg_obT")
            op("sp", lambda e: e.dma_start(out=dbg, in_=obT[:, :, :].rearrange("p k n -> p (k n)")),
               r=["obT_%d_%d" % (a, b) for a in range(8) for b in range(4)], dma="dbg")
            return
        S.barrier()

        S.muted = False
        self.off = mark3
        w5 = {}
        for nm, src in (("wga", win_v[:, :, O_GA:O_GA + D]), ("wgb", win_v[:, :, O_GB:O_GB + D]),
                        ("wpa", w_pa.rearrange("(kc p) n -> p kc n", p=128)), ("wpb", w_pb.rearrange("(kc p) n -> p kc n", p=128)),
                        ("wo", w_out.rearrange("(kc p) n -> p kc n", p=128))):
            w5[nm] = self.sb(nm, [128, KC, D], BF16)
            op("pool", lambda e, nm=nm, src=src: e.dma_start(out=w5[nm][:, :, :], in_=src), w=[nm], dma=nm)
        t5 = self.sb("t5", [128, D], F32)
        mT = self.sb("mT", [128, KC, 128], BF16)
        xs5 = [self.sb("xs5", [128, D], F32)]
        xo5 = xs5[0]
        m1 = gp_row[0]
        self.off = mark + 32768
        sga = self.sb("sga", [128, D], BF16)
        mb = self.sb("mb", [128, D], BF16)
        junk5 = sga
        for ti in range(NT):
            cs = slice(ti * 128, (ti + 1) * 128)
            for gi, (gw, pw, srcT, skey) in enumerate((("wga", "wpa", oaT, "oaT_%d" % ti), ("wgb", "wpb", obT, None))):
                for half in range(2):
                    hsl = slice(half * 512, (half + 1) * 512)
                    for kc in range(KC):
                        op("pe", lambda e, kc=kc, half=half, hsl=hsl, gw=gw, cs=cs: e.matmul(
                            ps[half][:, :], lhsT=h2T[:, kc, cs], rhs=w5[gw][:, kc, hsl], start=(kc == 0), stop=(kc == KC - 1)),
                           r=[gw, "h2T_%d" % ti], w=[psk[half]])
                    op("act", lambda e, half=half, hsl=hsl: e.activation(out=sga[:, hsl], in_=ps[half][:, :], func=AF.Sigmoid),
                       r=[psk[half]], w=["sga%d" % half])
                    rk = [pw] + ([skey] if skey else ["obT_%d_%d" % (a, ti // 4) for a in range(8)])
                    for kc in range(KC):
                        op("pe", lambda e, kc=kc, half=half, hsl=hsl, pw=pw, srcT=srcT, cs=cs: e.matmul(
                            ps[2 + half][:, :], lhsT=srcT[:, kc, cs], rhs=w5[pw][:, kc, hsl], start=(kc == 0), stop=(kc == KC - 1)),
                           r=rk, w=[psk[2 + half]])
                    if gi == 0:
                        op("dve", lambda e, half=half, hsl=hsl: e.tensor_tensor(out=m1[:, hsl], in0=ps[2 + half][:, :], in1=sga[:, hsl], op=ALU.mult),
                           r=[psk[2 + half], "sga%d" % half], w=["m1_%d" % half])
                    else:
                        op("dve", lambda e, half=half, hsl=hsl: e.tensor_tensor(out=t5[:, hsl], in0=ps[2 + half][:, :], in1=sga[:, hsl], op=ALU.mult),
                           r=[psk[2 + half], "sga%d" % half], w=["t5x"])
                        op("pool", lambda e, half=half, hsl=hsl: e.tensor_tensor(out=mb[:, hsl], in0=m1[:, hsl], in1=t5[:, hsl], op=ALU.add),
                           r=["m1_%d" % half, "t5x"], w=["mb%d" % half])
            p6 = ps[6][:, 0:512].bitcast(BF16)
            for kc in range(KC):
                op("pe", lambda e, kc=kc: e.transpose(out=p6[:, kc * 128:(kc + 1) * 128], in_=mb[:, kc * 128:(kc + 1) * 128], identity=ident_b),
                   r=["mb0", "mb1", "cstb"], w=[psk[6]])
            op("act", lambda e: e.copy(out=mT[:, :, :].rearrange("p a b -> p (a b)"), in_=p6), r=[psk[6]], w=["mT"])
            for half in range(2):
                hsl = slice(half * 512, (half + 1) * 512)
                for kc in range(KC):
                    op("pe", lambda e, kc=kc, half=half, hsl=hsl: e.matmul(ps[4 + half][:, :], lhsT=mT[:, kc, :], rhs=w5["wo"][:, kc, hsl],
                                                                         start=(kc == 0), stop=(kc == KC - 1)), r=["mT", "wo"], w=[psk[4 + half]])
            post_res(4, 5, 1, ti, x1d, x2d, junk5, t5, "t5x", xs5, xo5, "xs0", None)
        if self.stop_after == "mix":
            return
        S.barrier()

        B2 = alloc_ffn()
        ffn(2, 1, x2d, out, None, B2)


_CACHE = {}


def _consts():
    ident = np.eye(128, dtype=np.float32)
    U = np.triu(np.ones((128, 128), np.float32))
    ones = np.ones((128, 128), np.float32)
    e127 = np.zeros((128, 128), np.float32)
    e127[127, :] = 1.0
    return np.concatenate([ident, U, ones, e127], axis=1)


def make_in_maps(inputs):
    f = lambda a: np.ascontiguousarray(np.asarray(a, dtype=np.float32))
    x = f(inputs["x"])
    maps = []
    shared = {
        "w_ada": f(inputs["w_ada"][0]), "b_ada": f(inputs["b_ada"]).reshape(1, -1),
        "g_pre": f(inputs["g_pre"]).reshape(1, -1), "g_post": f(inputs["g_post"]).reshape(1, -1),
        "w_gu1": f(inputs["w_gu1"][0]), "w_dn1": f(inputs["w_dn1"][0]),
        "w_gu2": f(inputs["w_gu2"][0]), "w_dn2": f(inputs["w_dn2"][0]),
        "w_in": f(inputs["w_in"][0]), "w_a2": f(inputs["w_a2"][0]), "b_a": f(inputs["b_a"]).reshape(1, -1),
        "b_f": f(inputs["b_f"]).reshape(1, -1), "g_gla": f(inputs["g_gla"]).reshape(1, -1),
        "w_pa": f(inputs["w_pa"][0]), "w_pb": f(inputs["w_pb"][0]), "w_out": f(inputs["w_out"][0]),
        "consts": _consts(),
    }
    c = f(inputs["c"])
    for core in range(NCORES):
        b, j = core // GRP, core % GRP
        meta = np.zeros((128, 8), np.float32)
        for p in range(4):
            meta[:, p] = 1.0 if p < j else 0.0
            meta[:, 4 + p] = 0.0 if p < j else KILL
        m = dict(shared)
        m["x"] = np.ascontiguousarray(x[b, j * NTOK:(j + 1) * NTOK, :])
        m["c"] = np.ascontiguousarray(c[b:b + 1, :])
        m["meta"] = meta
        maps.append(m)
    return maps


def kernel(**inputs):
    if "nc" not in _CACHE:
        _CACHE["nc"] = Builder().build()
    nc = _CACHE["nc"]
    maps = make_in_maps(inputs)
    res = run_bass_kernel_spmd(nc, maps, core_ids=list(range(NCORES)))
    out = np.zeros((2, 8192, D), np.float32)
    for core in range(NCORES):
        b, j = core // GRP, core % GRP
        out[b, j * NTOK:(j + 1) * NTOK, :] = res.results[core]["out"]
    return out
```
